# Optimizing a Trainium2 kernel written in Bass

```python
import jax, jax.numpy as jnp
from jax import lax
import numpy as np

D_MODEL = 1024
BATCH = 32
SEQ = 256
DEPTH = 2
DEC_BATCH = 4
DEC_SEQ = 2048
PAST_LEN = 512

GRID_W = 64
N_EVEN = (DEPTH + 1) // 2
N_ODD = DEPTH // 2
EPS = 1e-6
H_RET = 4
DK_RET = D_MODEL // 8
DV_RET = 2 * DK_RET
RET_CHUNK = 128
POOL_WINDOWS = (2, 4, 8, 16)
N_POOL_GROUPS = 4
POOL_WIDTH = D_MODEL // 2
POOL_GROUP_DIM = POOL_WIDTH // N_POOL_GROUPS
EVEN_IN = 2 * H_RET * DK_RET + 2 * H_RET * DV_RET + POOL_WIDTH
EVEN_MIX = H_RET * DV_RET + POOL_WIDTH
HEAD_DIM = 128
N_HEADS = D_MODEL // HEAD_DIM
KV_HEADS = 2
Q_BLOCK = 128
ROPE_BASE = 10000.0
ROPE_PAIRS = HEAD_DIM // 4
QKV_OUT = (N_HEADS + 2 * KV_HEADS) * HEAD_DIM
D_FF = -(-8 * D_MODEL // (3 * 256)) * 256

kernel_name = "hybrid_retpool_gqa_diffusion_step"


def rmsnorm(x, gain):
    x32 = x.astype(jnp.float32)
    y = x32 * lax.rsqrt(jnp.mean(x32 * x32, axis=-1, keepdims=True) + EPS)
    return y.astype(x.dtype) * gain


def swiglu(h, w_in, w_out):
    a, b = jnp.split(h @ w_in, 2, axis=-1)
    return (jax.nn.silu(a) * b) @ w_out


def retention_chunked(q, k, v, log_gamma, s0):
    B, T, H, DK = q.shape
    DV = v.shape[-1]
    nc = T // RET_CHUNK
    dt = q.dtype
    n = jnp.arange(RET_CHUNK, dtype=jnp.float32)
    diff = n[:, None] - n[None, :]
    dmask = jnp.where(diff[None] >= 0, jnp.exp(jnp.maximum(diff, 0.0)[None] * log_gamma[:, None, None]), 0.0).astype(dt)
    xi = jnp.exp((n + 1.0)[:, None] * log_gamma[None, :]).astype(dt)
    zeta = jnp.exp((RET_CHUNK - 1.0 - n)[:, None] * log_gamma[None, :]).astype(dt)
    chunk_decay = jnp.exp(RET_CHUNK * log_gamma).astype(dt)

    def to_chunks(a):
        return a.reshape(B, nc, RET_CHUNK, H, a.shape[-1]).transpose(1, 0, 2, 3, 4)

    def step(s, inp):
        qc, kc, vc = inp
        scores = jnp.einsum('bnhd,bmhd->bhnm', qc, kc) * dmask[None]
        inner = jnp.einsum('bhnm,bmhe->bnhe', scores, vc)
        cross = jnp.einsum('bnhd,bhde->bnhe', qc * xi[None, :, :, None], s)
        s_new = s * chunk_decay[None, :, None, None] + jnp.einsum('bmhd,bmhe->bhde', kc * zeta[None, :, :, None], vc)
        return s_new, inner + cross

    s_fin, out = lax.scan(step, s0.astype(dt), (to_chunks(q), to_chunks(k), to_chunks(v)))
    out = out.transpose(1, 0, 2, 3, 4).reshape(B, T, H, DV)
    return out, s_fin


def multiscale_pool(u, pool_w, pool_scale):
    B, T, _ = u.shape
    ug = u.reshape(B, T, N_POOL_GROUPS, POOL_GROUP_DIM)
    csum = jnp.cumsum(ug.astype(jnp.float32), axis=1)
    P = jnp.concatenate([jnp.zeros((B, 1, N_POOL_GROUPS, POOL_GROUP_DIM), jnp.float32), csum], axis=1)
    half = jnp.array([w // 2 for w in POOL_WINDOWS], jnp.int32)
    t = jnp.arange(T, dtype=jnp.int32)
    lo = jnp.clip(t[:, None] - half[None, :], 0, T)
    hi = jnp.clip(t[:, None] + half[None, :], 0, T)
    gidx = jnp.arange(N_POOL_GROUPS, dtype=jnp.int32)[None, :]
    cnt = (hi - lo).astype(jnp.float32)[None, :, :, None]
    mean = (P[:, hi, gidx, :] - P[:, lo, gidx, :]) / cnt
    d = mean.astype(u.dtype) - ug
    out = jnp.einsum('btgc,gcd->btgd', d, pool_w).reshape(B, T, POOL_WIDTH)
    return out * pool_scale


def ret_pool_mixer(h, w_in, decay_logit, gn_gain, pool_w, pool_scale, w_out, s0_f, s0_b):
    B, T, _ = h.shape
    proj = h @ w_in
    sizes = np.cumsum([H_RET * DK_RET, H_RET * DK_RET, H_RET * DV_RET, H_RET * DV_RET])
    q, k, v, g, u = jnp.split(proj, [int(s) for s in sizes], axis=-1)
    q = q.reshape(B, T, H_RET, DK_RET) * (DK_RET ** -0.5)
    k = k.reshape(B, T, H_RET, DK_RET)
    v = v.reshape(B, T, H_RET, DV_RET)
    lg = jax.nn.log_sigmoid(decay_logit.astype(jnp.float32))
    o_f, s_f = retention_chunked(q, k, v, lg[0], s0_f)
    o_b, s_b = retention_chunked(q[:, ::-1], k[:, ::-1], v[:, ::-1], lg[1], s0_b)
    y = (o_f + o_b[:, ::-1]).astype(jnp.float32)
    mu = jnp.mean(y, axis=-1, keepdims=True)
    var = jnp.mean(jnp.square(y - mu), axis=-1, keepdims=True)
    y = ((y - mu) * lax.rsqrt(var + EPS)).astype(h.dtype).reshape(B, T, H_RET * DV_RET) * gn_gain
    ret_out = jax.nn.silu(g) * y
    pool_out = multiscale_pool(u, pool_w, pool_scale)
    out = jnp.concatenate([ret_out, pool_out], axis=-1) @ w_out
    return out, s_f, s_b


def axial_rope(x):
    T = x.shape[1]
    rows = T // GRID_W
    r = jnp.repeat(jnp.arange(rows, dtype=jnp.float32), GRID_W)
    cl = jnp.tile(jnp.arange(GRID_W, dtype=jnp.float32), rows)
    freqs = ROPE_BASE ** (-jnp.arange(ROPE_PAIRS, dtype=jnp.float32) / ROPE_PAIRS)

    def rot(xh, pos):
        ang = pos[:, None] * freqs[None, :]
        cos = jnp.cos(ang).astype(x.dtype)[None, :, None, :]
        sin = jnp.sin(ang).astype(x.dtype)[None, :, None, :]
        x1, x2 = xh[..., :ROPE_PAIRS], xh[..., ROPE_PAIRS:]
        return jnp.concatenate([x1 * cos - x2 * sin, x2 * cos + x1 * sin], axis=-1)

    hd2 = HEAD_DIM // 2
    return jnp.concatenate([rot(x[..., :hd2], r), rot(x[..., hd2:], cl)], axis=-1)


def block_attention(q, k, v):
    B, S, _, D = q.shape
    G = N_HEADS // KV_HEADS
    nb = S // Q_BLOCK
    qb = q.reshape(B, nb, Q_BLOCK, KV_HEADS, G, D).transpose(1, 0, 2, 3, 4, 5)
    scale = HEAD_DIM ** -0.5

    def one_block(qblk):
        s = jnp.einsum('bqkgd,btkd->bkgqt', qblk, k).astype(jnp.float32) * scale
        p = jax.nn.softmax(s, axis=-1).astype(v.dtype)
        return jnp.einsum('bkgqt,btkd->bqkgd', p, v)

    out = lax.map(one_block, qb)
    return out.transpose(1, 0, 2, 3, 4, 5).reshape(B, S, N_HEADS * D)


def attn_project(h, w_qkv, q_gain, k_gain):
    B, T, _ = h.shape
    q, k, v = jnp.split(h @ w_qkv, [N_HEADS * HEAD_DIM, (N_HEADS + KV_HEADS) * HEAD_DIM], axis=-1)
    q = rmsnorm(q.reshape(B, T, N_HEADS, HEAD_DIM), q_gain)
    k = rmsnorm(k.reshape(B, T, KV_HEADS, HEAD_DIM), k_gain)
    v = v.reshape(B, T, KV_HEADS, HEAD_DIM)
    return q, k, v


def setup_inputs(seed: int = 0) -> dict:
    key = jax.random.key(seed)
    ks = jax.random.split(key, 24)
    nrm = lambda k, shape, s: jax.random.normal(k, shape, jnp.float32) * s
    base_gamma = 1.0 - 2.0 ** (-5.0 - jnp.arange(H_RET, dtype=jnp.float32))
    base_logit = jnp.log(base_gamma / (1.0 - base_gamma))
    return {
        "x_prompt": nrm(ks[0], (BATCH, SEQ, D_MODEL), 1.0),
        "x_sample": nrm(ks[1], (DEC_BATCH, DEC_SEQ, D_MODEL), 1.0),
        "state_ret": nrm(ks[2], (DEC_BATCH, N_EVEN, 2, H_RET, DK_RET, DV_RET), 0.5),
        "cache_k": nrm(ks[3], (DEC_BATCH, N_ODD, PAST_LEN, KV_HEADS, HEAD_DIM), 1.0),
        "cache_v": nrm(ks[4], (DEC_BATCH, N_ODD, PAST_LEN, KV_HEADS, HEAD_DIM), 1.0),
        "c": nrm(ks[5], (DEC_BATCH, D_MODEL), 1.0),
        "c_ctx": nrm(ks[6], (D_MODEL,), 1.0),
        "w_ada": nrm(ks[7], (DEPTH, D_MODEL, 6 * D_MODEL), 0.5 * D_MODEL ** -0.5),
        "b_ada": nrm(ks[8], (DEPTH, 6 * D_MODEL), 0.02),
        "norm_gain": 1.0 + nrm(ks[9], (DEPTH, 4, D_MODEL), 0.05),
        "w_ffn_in": nrm(ks[10], (DEPTH, D_MODEL, 2 * D_FF), D_MODEL ** -0.5),
        "w_ffn_out": nrm(ks[11], (DEPTH, D_FF, D_MODEL), D_FF ** -0.5),
        "w_in_even": nrm(ks[12], (N_EVEN, D_MODEL, EVEN_IN), D_MODEL ** -0.5),
        "ret_decay_logit": base_logit[None, None, :] + nrm(ks[13], (N_EVEN, 2, H_RET), 0.1),
        "ret_gn_gain": 1.0 + nrm(ks[14], (N_EVEN, H_RET * DV_RET), 0.05),
        "pool_w": nrm(ks[15], (N_EVEN, N_POOL_GROUPS, POOL_GROUP_DIM, POOL_GROUP_DIM), POOL_GROUP_DIM ** -0.5),
        "pool_scale": 1.0 + nrm(ks[16], (N_EVEN, POOL_WIDTH), 0.1),
        "w_out_even": nrm(ks[17], (N_EVEN, EVEN_MIX, D_MODEL), EVEN_MIX ** -0.5),
        "w_qkv": nrm(ks[18], (N_ODD, D_MODEL, QKV_OUT), D_MODEL ** -0.5),
        "q_norm_gain": 1.0 + nrm(ks[19], (N_ODD, HEAD_DIM), 0.05),
        "k_norm_gain": 1.0 + nrm(ks[20], (N_ODD, HEAD_DIM), 0.05),
        "w_o": nrm(ks[21], (N_ODD, N_HEADS * HEAD_DIM, D_MODEL), (N_HEADS * HEAD_DIM) ** -0.5),
    }


def reference(x_prompt, x_sample, state_ret, cache_k, cache_v, c, c_ctx, w_ada, b_ada, norm_gain,
              w_ffn_in, w_ffn_out, w_in_even, ret_decay_logit, ret_gn_gain, pool_w, pool_scale,
              w_out_even, w_qkv, q_norm_gain, k_norm_gain, w_o):
    xp, xs = x_prompt, x_sample
    new_ret, new_k, new_v = [], [], []
    for i in range(DEPTH):
        j = i // 2
        mod_p = (jax.nn.silu(c_ctx)[None, :] @ w_ada[i] + b_ada[i])[:, None, :]
        mod_s = (jax.nn.silu(c) @ w_ada[i] + b_ada[i])[:, None, :]
        sh1p, sc1p, g1p, sh2p, sc2p, g2p = jnp.split(mod_p, 6, axis=-1)
        sh1s, sc1s, g1s, sh2s, sc2s, g2s = jnp.split(mod_s, 6, axis=-1)
        hp = rmsnorm(xp, norm_gain[i, 0]) * (1.0 + sc1p) + sh1p
        hs = rmsnorm(xs, norm_gain[i, 0]) * (1.0 + sc1s) + sh1s
        if i % 2 == 0:
            zeros = jnp.zeros((xp.shape[0], H_RET, DK_RET, DV_RET), xp.dtype)
            op, sf, sb = ret_pool_mixer(hp, w_in_even[j], ret_decay_logit[j], ret_gn_gain[j], pool_w[j],
                                        pool_scale[j], w_out_even[j], zeros, zeros)
            new_ret.append(jnp.stack([sf, sb], axis=1))
            os_, _, _ = ret_pool_mixer(hs, w_in_even[j], ret_decay_logit[j], ret_gn_gain[j], pool_w[j],
                                       pool_scale[j], w_out_even[j], state_ret[:, j, 0], state_ret[:, j, 1])
        else:
            qp, kp, vp = attn_project(hp, w_qkv[j], q_norm_gain[j], k_norm_gain[j])
            op = block_attention(qp, kp, vp) @ w_o[j]
            new_k.append(kp)
            new_v.append(vp)
            qs, ksl, vs = attn_project(hs, w_qkv[j], q_norm_gain[j], k_norm_gain[j])
            qs, ksl = axial_rope(qs), axial_rope(ksl)
            k_all = jnp.concatenate([cache_k[:, j], ksl], axis=1)
            v_all = jnp.concatenate([cache_v[:, j], vs], axis=1)
            os_ = block_attention(qs, k_all, v_all) @ w_o[j]
        xp = xp + g1p * rmsnorm(op, norm_gain[i, 1])
        xs = xs + g1s * rmsnorm(os_, norm_gain[i, 1])
        fp = swiglu(rmsnorm(xp, norm_gain[i, 2]) * (1.0 + sc2p) + sh2p, w_ffn_in[i], w_ffn_out[i])
        fs = swiglu(rmsnorm(xs, norm_gain[i, 2]) * (1.0 + sc2s) + sh2s, w_ffn_in[i], w_ffn_out[i])
        xp = xp + g2p * rmsnorm(fp, norm_gain[i, 3])
        xs = xs + g2s * rmsnorm(fs, norm_gain[i, 3])
    new_state_ret = jnp.stack(new_ret, axis=1)
    new_cache_k = jnp.stack(new_k, axis=1)
    new_cache_v = jnp.stack(new_v, axis=1)
    return (xp, xs, new_state_ret, new_cache_k, new_cache_v)
```

```python
import contextlib
import numpy as np
import concourse.bass as bass
import concourse.mybir as mybir
from concourse.bass_utils import run_bass_kernel_spmd

dt = mybir.dt
AF = mybir.ActivationFunctionType
ALU = mybir.AluOpType
AX = mybir.AxisListType
F32 = dt.float32
BF16 = dt.bfloat16
U8 = dt.uint8
ESZ = {dt.float32: 4, dt.bfloat16: 2, dt.uint8: 1, dt.int32: 4, dt.float16: 2}

D = 1024
DFF = 2816
NFC = 22
EPS = 1e-6
NPAST = 4


def _region(ap):
    a = ap.ap
    row, npart = a[0]
    off = ap.offset
    if row:
        p0, c0 = divmod(off, row)
    else:
        p0, c0 = 0, off
    ext = 1
    for st, cnt in a[1:]:
        ext += (cnt - 1) * abs(st)
    es = ESZ[ap.dtype]
    return (ap.tensor.name, p0, p0 + npart, c0 * es, (c0 + ext) * es)


class Op:
    __slots__ = ("eng", "fn", "deps", "sig", "tick", "dkey", "dcnt", "waits", "dwaits", "dsnap")

    def __init__(self, eng, fn, dkey=None):
        self.eng = eng
        self.fn = fn
        self.deps = set()
        self.sig = False
        self.tick = 0
        self.dkey = dkey
        self.dcnt = 0
        self.waits = None
        self.dwaits = None


class Prog:
    ENGS = ("pe", "act", "dve", "pool", "sp")

    def __init__(self):
        self.ops = []
        self.recs = {}
        self.dcount = {}

    def op(self, eng, fn, reads=(), writes=(), dkey=None):
        o = Op(eng, fn, dkey)
        if dkey is not None:
            self.dcount[dkey] = self.dcount.get(dkey, 0) + 1
            o.dcnt = self.dcount[dkey]
        for ap in reads:
            self._access(o, ap, "R")
        for ap in writes:
            self._access(o, ap, "W")
        o.dsnap = {d.dkey: self.dcount[d.dkey] - (1 if d.dkey == dkey else 0) for d in o.deps if d.dkey is not None}
        self.ops.append(o)
        return o

    def _access(self, o, ap, kind):
        if ap is None or isinstance(ap, (int, float)):
            return
        sp = str(ap.space)
        if "SB" not in sp and "PSUM" not in sp:
            return
        name, plo, phi, lo, hi = _region(ap)
        ispsum = "PSUM" in sp
        if ispsum:
            lo = (lo // 2048) * 2048
            hi = ((hi + 2047) // 2048) * 2048
            plo, phi = 0, 128
        L = self.recs.get(name)
        if L is None:
            L = self.recs[name] = []
        newL = []
        isdma = o.dkey is not None
        for r in L:
            rplo, rphi, rlo, rhi, rkind, rop = r
            if rlo < hi and lo < rhi and rplo < phi and plo < rphi:
                if kind == "W" or rkind == "W" or (ispsum and rop.eng != o.eng):
                    if rop is not o:
                        o.deps.add(rop)
                    if kind == "W" and lo <= rlo and rhi <= hi and plo <= rplo and rphi <= phi:
                        continue
                    if kind == "R" and rkind == "W" and rop.dkey is not None and not isdma:
                        newL.append((rplo, rphi, rlo, rhi, "W", o))
                        continue
                elif (not isdma) and rop.dkey is None and rop.eng == o.eng and rlo == lo and rhi == hi \
                        and rplo == plo and rphi == phi:
                    continue
            newL.append(r)
        newL.append((plo, phi, lo, hi, kind, o))
        self.recs[name] = newL

    def finalize(self):
        for o in self.ops:
            for d in o.deps:
                if d.dkey is None:
                    if d.eng == "pe" and o.eng == "pe" and o.dkey is None:
                        continue
                    d.sig = True
        cnt = {e: 0 for e in self.ENGS}
        for o in self.ops:
            if o.dkey is None and o.sig:
                cnt[o.eng] += 1
                o.tick = cnt[o.eng]
        known = {e: {x: 0 for x in self.ENGS} for e in self.ENGS}
        dknown = {e: {} for e in self.ENGS}
        tick_clock = {e: {} for e in self.ENGS}
        for o in self.ops:
            need = {}
            dneed = {}
            for d in o.deps:
                if d.dkey is not None:
                    dneed[d.dkey] = max(dneed.get(d.dkey, 0), o.dsnap[d.dkey])
                else:
                    if d.eng == "pe" and o.eng == "pe" and o.dkey is None:
                        continue
                    need[d.eng] = max(need.get(d.eng, 0), d.tick)
            kn = known[o.eng]
            waits = []
            for e, t in need.items():
                if kn[e] < t:
                    waits.append((e, t))
                    kn[e] = t
                    ck = tick_clock[e].get(t)
                    if ck:
                        for x, v in ck.items():
                            if kn[x] < v:
                                kn[x] = v
            dwaits = []
            dk = dknown[o.eng]
            for k, c in dneed.items():
                if dk.get(k, 0) < c:
                    dwaits.append((k, c))
                    dk[k] = c
            o.waits = waits
            o.dwaits = dwaits
            if o.dkey is None and o.sig:
                tick_clock[o.eng][o.tick] = dict(kn)


def emit(prog, nc, es, final_dma_keys=()):
    engs = Prog.ENGS
    sems = {e: es.enter_context(nc.semaphore("tick_" + e)) for e in engs if e != "sp"}
    dsems = {k: es.enter_context(nc.semaphore("d_" + k)) for k in prog.dcount}
    prog.finalize()
    block = es.enter_context(nc.Block())
    per = {e: [o for o in prog.ops if o.eng == e] for e in engs}

    def run(e, eng):
        for o in per[e]:
            for (we, t) in o.waits:
                eng.wait_ge(sems[we], t)
            for (k, c) in o.dwaits:
                eng.wait_ge(dsems[k], 16 * c)
            ins = o.fn(eng)
            if o.dkey is not None:
                ins.then_inc(dsems[o.dkey], 16)
            elif o.sig:
                ins.then_inc(sems[e], 1)
        if e == "sp":
            for k in final_dma_keys:
                eng.wait_ge(dsems[k], 16 * prog.dcount[k])

    @block.tensor
    def _(eng):
        run("pe", eng)

    @block.scalar
    def _(eng):
        run("act", eng)

    @block.vector
    def _(eng):
        run("dve", eng)

    @block.gpsimd
    def _(eng):
        run("pool", eng)

    @block.sync
    def _(eng):
        run("sp", eng)


def build(NT, dbg=False):
    T = NT * 128
    NKC = NPAST + NT
    NSEQ = NT // 2
    NQB = T // 512
    assert T % 1024 == 0 or T == 512
    nc = bass.Bass("TRN2", target_bir_lowering=False)
    es = contextlib.ExitStack()
    P = Prog()

    def din(name, shape):
        return nc.dram_tensor(name, list(shape), F32, kind="ExternalInput").ap()

    def dout(name, shape):
        return nc.dram_tensor(name, list(shape), F32, kind="ExternalOutput").ap()

    x_d = din("x", [T, D])
    cc_d = din("cc", [128, 8])
    s0_d = din("s0", [2, 4, 128, 256])
    ck_d = din("ck", [512, 256])
    cv_d = din("cv", [512, 256])
    wada_d = din("w_ada", [2, D, 6 * D])
    bada_d = din("b_ada", [2, 6 * D])
    ngain_d = din("norm_gain", [2, 4 * D])
    wfi_d = din("w_ffn_in", [2, D, 2 * DFF])
    wfo_d = din("w_ffn_out", [2, DFF, D])
    win_d = din("w_in_even", [D, 3584])
    dlog_d = din("dlog", [1, 8])
    gn_d = din("gn", [128, 8])
    poolw_d = din("pool_w", [4, 128, 128])
    pscale_d = din("pscale", [128, 4])
    woe_d = din("w_out_even", [1536, D])
    wqkv_d = din("w_qkv", [D, 1536])
    qkg_d = din("qkg", [1, 256])
    wo_d = din("w_o", [D, D])
    ident_d = din("ident", [128, 128])
    retE_d = din("retE", [128, 4, 128])
    retZ_d = din("retZ", [128, 2])
    poolB_d = din("poolB", [128, 5, 512])
    poolC_d = din("poolC", [128, 3, 512])
    flags_d = din("flags", [128, 4, NT])
    maskb_d = din("maskb", [128, NKC * NSEQ])
    rope_d = din("rope", [T, 256])

    y_d = dout("y", [T, D])
    st_d = dout("st", [NSEQ, 2, 4, 128, 256])
    nk_d = dout("nk", [T, 256])
    nv_d = dout("nv", [T, 256])
    dbg_outs = {}

    with es:
        ARENA = 198656 + 8192
        arena = es.enter_context(nc.sbuf_tensor("arena", [128, ARENA], U8))
        ps = es.enter_context(nc.psum_tensor("ps", [128, 8, 512], F32))

        def sbt(name, shape, dtype):
            return es.enter_context(nc.sbuf_tensor(name, list(shape), dtype))

        def A(off, shape, dtype, p0=0, p1=128):
            n = 1
            for s in shape[1:]:
                n *= s
            nb = n * ESZ[dtype]
            assert off % 4 == 0 and off + nb <= ARENA, (off, nb)
            v = arena[p0:p1, off:off + nb].bitcast(dtype)
            if len(shape) == 3:
                v = v.rearrange("p (a b) -> p a b", a=shape[1])
            elif len(shape) == 4:
                v = v.rearrange("p (a b c) -> p a b c", a=shape[1], b=shape[2])
            return v

        ident = sbt("ident_sb", [128, 128], BF16)
        ones_bf = sbt("ones_bf", [128, 128], BF16)
        one1 = sbt("one1", [1, 128], F32)
        pps = [sbt("pp_sb%d" % i_, [128, 4, 8], F32) for i_ in range(2)]
        ggcs = [sbt("ggc_sb%d" % i_, [128, 2, 8], F32) for i_ in range(2)]
        cur = {"pp": pps[0]}
        identf = sbt("identf", [128, 128], F32)
        onesf = sbt("onesf", [128, 128], F32)
        modT = sbt("modT", [128, 2, 48], F32)
        bnT = sbt("bnT", [128, 2, 80], F32)
        s_bf = sbt("s_bf", [128, 8], BF16)
        st8 = sbt("st8", [128, 64], F32)
        flags = sbt("flags_sb", [128, 4, NT], F32)
        maskb = sbt("maskb_sb", [128, NKC * NSEQ], F32)
        gnpp = sbt("gnpp", [128, 12], F32)
        ggb = A(198656, [128, 2, 1024], F32)
        X = A(0, [128, NT, 1024], F32)

        def psb(b):
            return ps[:, b, :]

        def psbf(b):
            return ps[:, b, :].bitcast(BF16)

        def MM(out, lhsT, rhs, start=True, stop=True):
            P.op("pe", lambda e: e.matmul(out, lhsT, rhs, start=start, stop=stop), reads=[lhsT, rhs], writes=[out])

        def TR(out, in_):
            P.op("pe", lambda e: e.transpose(out, in_, ident[:]), reads=[in_, ident[:]], writes=[out])

        def ACT(out, in_, func, bias=0.0, scale=1.0, accum=None):
            rd = [in_] + [a for a in (bias, scale) if not isinstance(a, (int, float))]
            wr = [out] + ([accum] if accum is not None else [])
            if accum is not None:
                P.op("act", lambda e: e.activation(out, in_, func, bias=bias, scale=scale, accum_out=accum), reads=rd, writes=wr)
            else:
                P.op("act", lambda e: e.activation(out, in_, func, bias=bias, scale=scale), reads=rd, writes=wr)

        def TT(out, a, b, op, eng="dve"):
            P.op(eng, lambda e: e.tensor_tensor(out, a, b, op), reads=[a, b], writes=[out])

        def TS(out, a, s1, s2, op0, op1=None, eng="dve"):
            rd = [a] + [s for s in (s1, s2) if s is not None and not isinstance(s, (int, float))]
            if op1 is None:
                P.op(eng, lambda e: e.tensor_scalar(out, a, s1, None, op0), reads=rd, writes=[out])
            else:
                P.op(eng, lambda e: e.tensor_scalar(out, a, s1, s2, op0, op1), reads=rd, writes=[out])

        def STT(out, a, s, b, op0, op1, eng="dve"):
            rd = [a, b] + ([] if isinstance(s, (int, float)) else [s])
            P.op(eng, lambda e: e.scalar_tensor_tensor(out, a, s, b, op0, op1), reads=rd, writes=[out])

        def CP(out, in_, eng="dve"):
            if eng == "act":
                P.op("act", lambda e: e.copy(out, in_), reads=[in_], writes=[out])
            else:
                P.op(eng, lambda e: e.tensor_copy(out, in_), reads=[in_], writes=[out])

        def RECIP(out, in_):
            P.op("dve", lambda e: e.reciprocal(out, in_), reads=[in_], writes=[out])

        def MEMSET(ap, v, eng="dve"):
            P.op(eng, lambda e: e.memset(ap, v), writes=[ap])

        def DMA(out, in_, key, eng="sp"):
            P.op(eng, lambda e: e.dma_start(out=out, in_=in_), reads=[in_], writes=[out], dkey=key)

        dbg_n = [0]

        def DBG(name, ap, shape):
            if not dbg:
                return
            assert NT <= 8
            o = dout("dbg_" + name, shape)
            dbg_outs[name] = o
            if ap.dtype != F32:
                tmp = A(32768, list(shape), F32)
                CP(tmp, ap)
                DMA(o, tmp, "dbg")
            else:
                DMA(o, ap, "dbg")

        DMA(ident[:], ident_d, "c0", eng="pool")
        MEMSET(ones_bf[:], 1.0)
        MEMSET(one1[:], 1.0)
        MEMSET(onesf[:], 1.0)
        DMA(identf[:], ident_d, "c1")
        DMA(flags[:], flags_d, "c1")
        DMA(maskb[:], maskb_d, "c1")
        DMA(gnpp[:, 0:8], gn_d, "c1")
        DMA(gnpp[:, 8:12], pscale_d, "c1")

        MS = 186368
        ccs = A(MS + 8192, [128, 8], F32)
        b48 = A(MS + 8448, [48, 128], F32, 0, 48)
        n32 = A(MS + 8960, [32, 128], F32, 0, 32)
        mod_state = {"nblk": 0, "inflight": None}

        def mod_init():
            DMA(ccs, cc_d, "m0")
            ACT(s_bf[:], ccs, AF.Silu)
            for i in range(2):
                DMA(b48, bada_d[i:i + 1, :].rearrange("o (c p) -> (o c) p", p=128), "mb")
                DMA(n32, ngain_d[i:i + 1, :].rearrange("o (c p) -> (o c) p", p=128), "mb")
                MM(ps[:, 7, 0:48], b48, identf[0:48, 0:48])
                MM(ps[:, 7, 64:96], n32, identf[0:32, 0:32])
                CP(bnT[:, i, 0:48], ps[:, 7, 0:48])
                CP(bnT[:, i, 48:80], ps[:, 7, 64:96])

        def mod_dma(i, col0, width, slot, key):
            DMA(slot, wada_d[i][:, col0:col0 + width].rearrange("(p c) f -> p c f", c=8), key, eng="pool")

        def mod_mm(i, col0, width, slot, bank, pc0):
            nf = width // 128
            f0 = col0 // 128
            for fc in range(nf):
                for c in range(8):
                    MM(ps[:, bank, pc0 + fc:pc0 + fc + 1], slot[:, c, fc * 128:(fc + 1) * 128], s_bf[:, c:c + 1],
                       start=(c == 0), stop=(c == 7))
            TT(modT[:, i, f0:f0 + nf], ps[:, bank, pc0:pc0 + nf], bnT[:, i, f0:f0 + nf], ALU.add)

        def mod_fin(i, part):
            pp_ = pps[i]
            m = modT[:, i, :]
            g = bnT[:, i, 48:80]
            if part == 0:
                STT(pp_[:, 0, :], m[:, 8:16], 1.0, g[:, 0:8], ALU.add, ALU.mult)
                CP(pp_[:, 1, :], m[:, 0:8])
            else:
                STT(pp_[:, 2, :], m[:, 32:40], 1.0, g[:, 16:24], ALU.add, ALU.mult)
                CP(pp_[:, 3, :], m[:, 24:32])
                TT(ggcs[i][:, 0, :], m[:, 16:24], g[:, 8:16], ALU.mult)
                TT(ggcs[i][:, 1, :], m[:, 40:48], g[:, 24:32], ALU.mult)

        def build_ggb(i, tmp_off):
            for v in range(2):
                for c in range(8):
                    dg = A(tmp_off + ((v * 8 + c) % 2) * 512, [128, 128], F32)
                    TS(dg, identf[:], ggcs[i][:, v, c:c + 1], None, ALU.mult)
                    b = 2 + (c // 4) + v * 2
                    MM(ps[:, b, (c % 4) * 128:(c % 4 + 1) * 128], onesf[:], dg)
                for hf in range(2):
                    CP(ggb[:, v, hf * 512:(hf + 1) * 512], psb(2 + hf + v * 2), eng="act")

        def pipeline(n, stages, oldest_first=False):
            ns = len(stages)
            for step in range(n + ns - 1):
                order = list(enumerate(stages))
                if oldest_first:
                    order = order[::-1]
                for s_i, fn in order:
                    it_ = step - s_i
                    if 0 <= it_ < n:
                        fn(it_)

        def norm_A1(src_tile, j, tmp_off):
            junk = A(tmp_off + 4096, [128, 1024], BF16)
            ms = st8[:, (j % 2) * 2:(j % 2) * 2 + 1]
            ACT(junk, src_tile, AF.Square, scale=1.0 / 32.0, accum=ms)

        def norm_A2(src_tile, j, tmp_off):
            xn = A(tmp_off + (j % 2) * 2048, [128, 1024], BF16)
            ms = st8[:, (j % 2) * 2:(j % 2) * 2 + 1]
            rs = st8[:, (j % 2) * 2 + 1:(j % 2) * 2 + 2]
            ACT(rs, ms, AF.Sqrt, bias=EPS)
            RECIP(rs, rs)
            TS(xn, src_tile, rs, None, ALU.mult)

        def skewed(stages, jj, n):
            for k, fn in enumerate(stages):
                if 0 <= jj - k < n:
                    fn(jj - k)

        def norm_B(hT_dst, j, vsel, tmp_off):
            xn = A(tmp_off + (j % 2) * 2048, [128, 1024], BF16)
            pts = (psbf((j % 2) * 2), psbf((j % 2) * 2 + 1))
            for c in range(8):
                TR(pts[c % 2][:, (c // 2) * 128:(c // 2 + 1) * 128], xn[:, c * 128:(c + 1) * 128])
            for c in range(8):
                src = pts[c % 2][:, (c // 2) * 128:(c // 2 + 1) * 128]
                if c % 2 == 0:
                    ACT(hT_dst[:, c, j * 128:(j + 1) * 128], src, AF.Identity,
                        scale=cur['pp'][:, vsel, c:c + 1], bias=cur['pp'][:, vsel + 1, c:c + 1])
                else:
                    TS(hT_dst[:, c, j * 128:(j + 1) * 128], src,
                       cur['pp'][:, vsel, c:c + 1], cur['pp'][:, vsel + 1, c:c + 1], ALU.mult, ALU.add)

        def post_norm(j, banks, v, tmp_off):
            tmp = A(tmp_off, [128, 1024], F32)
            junk = A(tmp_off + 4096 - 2048, [128, 512], F32) if False else None
            c0 = 8 + (j % 2) * 4
            for hf in range(2):
                ACT(tmp[:, hf * 512:(hf + 1) * 512], psb(banks[hf]), AF.Square, scale=1.0 / 32.0, accum=st8[:, c0 + hf:c0 + hf + 1])
            TT(st8[:, c0 + 2:c0 + 3], st8[:, c0:c0 + 1], st8[:, c0 + 1:c0 + 2], ALU.add)
            ACT(st8[:, c0 + 3:c0 + 4], st8[:, c0 + 2:c0 + 3], AF.Sqrt, bias=EPS)
            RECIP(st8[:, c0 + 3:c0 + 4], st8[:, c0 + 3:c0 + 4])
            for hf in range(2):
                sl = slice(hf * 512, (hf + 1) * 512)
                STT(tmp[:, sl], psb(banks[hf]), st8[:, c0 + 3:c0 + 4], ggb[:, v, sl], ALU.mult, ALU.mult)
                TT(X[:, j, sl], X[:, j, sl], tmp[:, sl], ALU.add, eng="pool")

        def ffn_layout(i):
            if i == 0:
                return dict(h2T=65536, hidT=81920, wout=126976, wins=172032, sil=188416, TMP=192512)
            return dict(h2T=155648, hidT=65536, wout=110592, wins=172032, sil=188416, TMP=192512)

        def ffn_norm_stages(i, th):
            L = ffn_layout(i)
            HT = min(T, 1024)
            ntile = HT // 128
            h2T = A(L["h2T"], [128, 8, HT], BF16)
            return [lambda jj: norm_A1(X[:, th * ntile + jj, :], jj, L["TMP"]),
                    lambda jj: norm_A2(X[:, th * ntile + jj, :], jj, L["TMP"]),
                    lambda jj: norm_B(h2T, jj, 2, L["TMP"])]

        ffn_pref = {}

        def ffn_win_dma(i, fg):
            L = ffn_layout(i)
            sl = A(L["wins"] + (fg % 2) * 8192, [128, 8, 2, 256], BF16)
            for part in range(2):
                c0 = part * DFF + fg * 256
                DMA(sl[:, :, part, :], wfi_d[i][:, c0:c0 + 256].rearrange("(c p) f -> p c f", p=128), "wfi%d" % (fg % 2), eng="pool")

        def ffn_prefetch(i):
            for fg in range(2):
                ffn_win_dma(i, fg)
            ffn_pref[i] = True

        def ffn(i):
            L = ffn_layout(i)
            HT = min(T, 1024)
            NH = T // HT
            ntile = HT // 128
            h2T = A(L["h2T"], [128, 8, HT], BF16)
            hidT = A(L["hidT"], [128, NFC, HT], BF16)
            wout = A(L["wout"], [128, NFC, 1024], BF16)
            wins = [A(L["wins"] + k * 8192, [128, 8, 2, 256], BF16) for k in range(2)]
            sil = [A(L["sil"] + k * 2048, [128, 512], F32) for k in range(2)]
            for th in range(NH):
                for fg in range(NFC // 2):
                    sl = wins[fg % 2]
                    if not (th == 0 and fg < 2 and ffn_pref.get(i)):
                        ffn_win_dma(i, fg)
                    if th == 0:
                        f0 = fg * 2
                        DMA(wout[:, f0:f0 + 2, :], wfo_d[i][f0 * 128:(f0 + 2) * 128, :].rearrange("(c p) n -> p c n", p=128), "wfo", eng="pool")
                    for fi in range(2):
                        f = fg * 2 + fi
                        for tb in range(HT // 512):
                            ba = 4 + ((f * 2 + tb) % 2) * 2
                            bb = ba + 1
                            tsl = slice(tb * 512, (tb + 1) * 512)
                            for c in range(8):
                                MM(psb(ba), sl[:, c, 0, fi * 128:(fi + 1) * 128], h2T[:, c, tsl], start=(c == 0), stop=(c == 7))
                            for c in range(8):
                                MM(psb(bb), sl[:, c, 1, fi * 128:(fi + 1) * 128], h2T[:, c, tsl], start=(c == 0), stop=(c == 7))
                            s_ = sil[(f * 2 + tb) % 2]
                            ACT(s_, psb(ba), AF.Silu)
                            TT(hidT[:, f, tsl], s_, psb(bb), ALU.mult)
                nst = ffn_norm_stages(i, th + 1) if th + 1 < NH else None
                for jj in range(ntile + 2):
                    if jj < ntile:
                        j = th * ntile + jj
                        banks = (4 + (jj % 2) * 2, 5 + (jj % 2) * 2)
                        for hf in range(2):
                            for f in range(NFC):
                                MM(psb(banks[hf]), hidT[:, f, jj * 128:(jj + 1) * 128], wout[:, f, hf * 512:(hf + 1) * 512],
                                   start=(f == 0), stop=(f == NFC - 1))
                        post_norm(j, banks, 1, L["sil"])
                    if nst is not None:
                        skewed(nst, jj, ntile)

        mod_init()
        m0slots = [A(k * 8192, [128, 8, 512], BF16) for k in range(4)]
        for cb in range(4):
            mod_dma(0, cb * 512, 512, m0slots[cb], "wa0_%d" % cb)
        for cb in range(4):
            mod_mm(0, cb * 512, 512, m0slots[cb], 7, 128 + cb * 4)
        mod_fin(0, 0)
        p0slots = [A(o_, [128, 8, 512], BF16) for o_ in (49152, 57344, 160768, 168960, 186368)]

        def mod0_rest(j):
            n_dma = j
            if 0 <= n_dma < 8:
                mod_dma(0, (4 + n_dma) * 512, 512, p0slots[n_dma % 5], "wa0_%d" % (4 + n_dma % 5))
            n_mm = j - 4
            if 0 <= n_mm < 8:
                mod_mm(0, (4 + n_mm) * 512, 512, p0slots[n_mm % 5], 7, 128 + (n_mm % 5) * 4)

        hT = A(65536, [128, 8, T], BF16)
        mixT = A(98304, [128, 12, T], BF16)
        LT = 160768
        NXI = 6
        xin = [A(32768 + k * 4096, [128, 1024], F32) for k in range(NXI)]
        def l0_A(j):
            DMA(xin[j % NXI], x_d[j * 128:(j + 1) * 128, :], "xin%d" % (j % NXI))
            norm_A1(xin[j % NXI], j, LT + 8192)
        def l0_B(j):
            norm_B(hT, j, 0, LT + 8192)
        pipeline(NT, [l0_A, lambda j: norm_A2(xin[j % NXI], j, LT + 8192), l0_B])
        DBG("hT0", hT[:, :, 0:128], [128, 8, 128])

        RC = LT + 16384
        lg = A(RC, [128, 8], F32)
        dlb = A(RC + 32, [128, 8], F32)
        dec = A(RC + 64, [128, 8], F32)
        retE = A(RC + 128, [128, 4, 128], F32)
        retZ = A(RC + 128 + 2048, [128, 2], F32)
        Mall = A(RC + 2304, [128, 4, 128], BF16)
        xif = A(RC + 3328, [128, 4, 128], F32)
        xib = A(RC + 5376, [128, 4, 128], F32)
        zf = A(RC + 7424, [128, 4], F32)
        zb = A(RC + 7440, [128, 4], F32)
        mtmp = A(RC + 7456, [128, 2, 128], F32)
        kd = A(RC + 8480, [128, 2, NT, 4], F32)
        DMA(dlb, dlog_d.partition_broadcast(128), "c2")
        DMA(retE, retE_d, "c2")
        DMA(retZ, retZ_d, "c2")
        ACT(lg, dlb, AF.Exp, scale=-1.0)
        ACT(lg, lg, AF.Ln, bias=1.0)
        TS(lg, lg, -1.0, None, ALU.mult)
        ACT(dec, lg, AF.Exp, scale=128.0)
        SQ = 128.0 ** -0.5
        for h in range(4):
            ACT(mtmp[:, 0, :], retE[:, 0, :], AF.Exp, scale=lg[:, h:h + 1])
            ACT(mtmp[:, 1, :], retE[:, 1, :], AF.Exp, scale=lg[:, 4 + h:5 + h])
            TT(mtmp[:, 0, :], mtmp[:, 0, :], mtmp[:, 1, :], ALU.add)
            TS(Mall[:, h, :], mtmp[:, 0, :], SQ, None, ALU.mult)
            ACT(xif[:, h, :], retE[:, 2, :], AF.Exp, scale=lg[:, h:h + 1])
            ACT(xib[:, h, :], retE[:, 3, :], AF.Exp, scale=lg[:, 4 + h:5 + h])
            ACT(zf[:, h:h + 1], retZ[:, 0:1], AF.Exp, scale=lg[:, h:h + 1])
            ACT(zb[:, h:h + 1], retZ[:, 1:2], AF.Exp, scale=lg[:, 4 + h:5 + h])
        TS(xif.rearrange("p a b -> p (a b)"), xif.rearrange("p a b -> p (a b)"), SQ, None, ALU.mult)
        TS(xib.rearrange("p a b -> p (a b)"), xib.rearrange("p a b -> p (a b)"), SQ, None, ALU.mult)
        for d_ in range(2):
            for c in range(NT):
                TS(kd[:, d_, c, :], dec[:, d_ * 4:d_ * 4 + 4], flags[:, 2 + d_, c:c + 1], None, ALU.mult)

        u_tok = A(0, [128, NT, 512], BF16)
        Wu = A(16384, [128, 8, 512], BF16)
        pB = A(24576, [128, 5, 512], BF16)
        pC = A(29696, [128, 3, 512], F32)
        pw = A(35840, [128, 4, 128], BF16)
        DMA(Wu, win_d[:, 3072:3584].rearrange("(c p) f -> p c f", p=128), "wu", eng="pool")
        DMA(pB, poolB_d, "wu", eng="pool")
        DMA(pC, poolC_d, "c4")
        DMA(pw, poolw_d.rearrange("g c d -> c g d"), "wu", eng="pool")
        for j in range(NT):
            b = 2 + j % 2
            for c in range(8):
                MM(psb(b), hT[:, c, j * 128:(j + 1) * 128], Wu[:, c, :], start=(c == 0), stop=(c == 7))
            CP(u_tok[:, j, :], psb(b), eng="act")
        PO = 36864
        for j in range(NT):
            k2 = j % 2
            Cc = A(PO + k2 * 3072, [128, 512], BF16)
            Cp = A(PO + k2 * 3072 + 1024, [128, 512], BF16)
            Cn = A(PO + k2 * 3072 + 2048, [128, 512], BF16)
            cnt = A(PO + 6144 + k2 * 2048, [128, 512], F32)
            dT = A(PO + 10240 + k2 * 1024, [128, 512], BF16)
            hp = flags[:, 0, j:j + 1]
            hn = flags[:, 1, j:j + 1]
            STT(Cc, pB[:, 1, :], hp, pB[:, 0, :], ALU.mult, ALU.add)
            STT(Cc, pB[:, 2, :], hn, Cc, ALU.mult, ALU.add)
            TS(Cp, pB[:, 3, :], hp, None, ALU.mult)
            TS(Cn, pB[:, 4, :], hn, None, ALU.mult)
            STT(cnt, pC[:, 1, :], hp, pC[:, 0, :], ALU.mult, ALU.add)
            STT(cnt, pC[:, 2, :], hn, cnt, ALU.mult, ALU.add)
            RECIP(cnt, cnt)
            jp = max(j - 1, 0)
            jn = min(j + 1, NT - 1)
            b = 4 + k2
            for g in range(4):
                gs = slice(g * 128, (g + 1) * 128)
                MM(ps[:, b, gs], u_tok[:, jp, gs], Cp[:, gs], start=True, stop=False)
                MM(ps[:, b, gs], u_tok[:, j, gs], Cc[:, gs], start=False, stop=False)
                MM(ps[:, b, gs], u_tok[:, jn, gs], Cn[:, gs], start=False, stop=True)
            TT(dT, psb(b), cnt, ALU.mult)
            b2 = 6
            mod0_rest(j)
            for g in range(4):
                gs = slice(g * 128, (g + 1) * 128)
                MM(ps[:, b2, gs], pw[:, g, :], dT[:, gs])
            CP(mixT[:, 8:12, j * 128:(j + 1) * 128], psb(b2).rearrange("p (g t) -> p g t", g=4), eng="act")
        DBG("poolT", mixT[:, 8:12, 0:128], [128, 4, 128])

        for j_ in range(NT, 12):
            mod0_rest(j_)
        mod_fin(0, 1)
        kvbufs = [A(32768, [128, NT, 384], BF16), A(186368, [128, NT, 384], BF16)]

        def wh_load(h):
            Wh = A((h % 2) * 12288, [128, 8, 768], BF16)
            for (c0, w_, o_) in ((h * 128, 128, 0), (512 + h * 128, 128, 128), (1024 + h * 256, 256, 256), (2048 + h * 256, 256, 512)):
                DMA(Wh[:, :, o_:o_ + w_], win_d[:, c0:c0 + w_].rearrange("(c p) f -> p c f", p=128), "wh%d" % (h % 2), eng="pool")

        def p1_items(h):
            Wh = A((h % 2) * 12288, [128, 8, 768], BF16)
            kv = kvbufs[h % 2]
            items = []
            for j in range(NT):
                def item(j=j):
                    b = 6 + (j % 2)
                    jsl = slice(j * 128, (j + 1) * 128)
                    for c in range(8):
                        MM(ps[:, b, 0:384], hT[:, c, jsl], Wh[:, c, 128:512], start=(c == 0), stop=(c == 7))
                    CP(kv[:, j, :], ps[:, b, 0:384], eng="act")
                items.append(item)
            return items

        kzb = A(147456, [128, T], BF16)
        wh_load(0)
        for it_ in p1_items(0):
            it_()
        TS(kzb.rearrange("p (c n) -> p c n", n=128), kvbufs[0][:, :, 0:128], zb[:, 0:1], None, ALU.mult)
        for h in range(4):
            Wh = A((h % 2) * 12288, [128, 8, 768], BF16)
            qT = A(24576, [128, T], BF16)
            kT = A(24576 + 2 * T, [128, T], BF16)
            kv = kvbufs[h % 2]
            sg = A(45056, [128, NT, 256], BF16)
            qxf = A(53248, [128, T], BF16)
            qxb = A(57344, [128, T], BF16)
            kzf = A(61440, [128, T], BF16)
            Sbs = A(151552, [128, NT, 256], BF16)
            SB0 = 159744
            so4 = [A(SB0 + k * 1024, [128, 256], F32) for k in range(4)]
            Sfbf = [A(SB0 + 4096 + k * 512, [128, 256], BF16) for k in range(2)]
            PTs = [A(SB0 + 5120 + k * 256, [128, 128], BF16) for k in range(3)]
            yns = [A(SB0 + 6144 + k * 1024, [128, 256], F32) for k in range(3)]
            rts = [A(SB0 + 9216 + k * 512, [128, 256], BF16) for k in range(2)]
            if h + 1 < 4:
                wh_load(h + 1)
            m1slots = [A(169984 + k * 4096, [128, 8, 256], BF16) for k in range(2)]
            if h > 0:
                for k in range(2):
                    mod_mm(1, ((h - 1) * 2 + k) * 256, 256, m1slots[k], 1, 128 + k * 2)
            for k in range(2):
                mod_dma(1, (h * 2 + k) * 256, 256, m1slots[k], "wa1_%d" % k)
            TS(kzf.rearrange("p (c n) -> p c n", n=128), kv[:, :, 0:128], zf[:, h:h + 1], None, ALU.mult)
            p2 = []
            for tb in range(T // 512):
                for (dst, o_) in ((qT, 0), (kT, 128)):
                    def item(tb=tb, dst=dst, o_=o_):
                        tsl = slice(tb * 512, (tb + 1) * 512)
                        b = 0
                        for c in range(8):
                            MM(psb(b), Wh[:, c, o_:o_ + 128], hT[:, c, tsl], start=(c == 0), stop=(c == 7))
                        CP(dst[:, tsl], psb(b), eng="act")
                    p2.append(item)
            for j in range(NT):
                def item(j=j):
                    b = 6 + (j % 2)
                    jsl = slice(j * 128, (j + 1) * 128)
                    for c in range(8):
                        MM(ps[:, b, 0:256], hT[:, c, jsl], Wh[:, c, 512:768], start=(c == 0), stop=(c == 7))
                    ACT(sg[:, j, :], ps[:, b, 0:256], AF.Silu)
                p2.append(item)
            DMA(so4[NT % 4], s0_d[1, h], "s0")
            npi = 0
            for i_ in range(NT):
                c = NT - 1 - i_
                csl = slice(c * 128, (c + 1) * 128)
                b = 2 + c % 2
                MM(ps[:, b, 0:256], kzb[:, csl], kv[:, c, 128:384])
                ACT(Sbs[:, c, :], so4[(c + 1) % 4], AF.Identity, scale=flags[:, 3, c:c + 1])
                STT(so4[c % 4], so4[(c + 1) % 4], kd[:, 1, c, h:h + 1], ps[:, b, 0:256], ALU.mult, ALU.add)
                if c % 2 == 0:
                    DMA(st_d[c // 2, 1, h], so4[c % 4], "so%d" % (c % 4))
                tgt = ((i_ + 1) * len(p2)) // NT
                while npi < tgt:
                    p2[npi]()
                    npi += 1
            TT(qxf.rearrange("p (c n) -> p c n", n=128), qT.rearrange("p (c n) -> p c n", n=128),
               xif[:, h:h + 1, :].to_broadcast([128, NT, 128]), ALU.mult)
            TT(qxb.rearrange("p (c n) -> p c n", n=128), qT.rearrange("p (c n) -> p c n", n=128),
               xib[:, h:h + 1, :].to_broadcast([128, NT, 128]), ALU.mult, eng="pool")
            DMA(so4[3], s0_d[0, h], "s0")
            ACT(Sfbf[0], so4[3], AF.Identity, scale=flags[:, 2, 0:1])

            def F0(c):
                csl = slice(c * 128, (c + 1) * 128)
                k2 = c % 2
                MM(ps[:, 2 + k2, 0:128], kT[:, csl], qT[:, csl])
                MM(ps[:, 2 + k2, 128:384], kzf[:, csl], kv[:, c, 128:384])
                TT(PTs[c % 3], ps[:, 2 + k2, 0:128], Mall[:, h, :], ALU.mult)

            def F1(c):
                csl = slice(c * 128, (c + 1) * 128)
                k2 = c % 2
                MM(ps[:, 4 + k2, 0:256], PTs[c % 3], kv[:, c, 128:384], start=True, stop=False)
                MM(ps[:, 4 + k2, 0:256], qxb[:, csl], Sbs[:, c, :], start=False, stop=False)
                MM(ps[:, 4 + k2, 0:256], qxf[:, csl], Sfbf[k2], start=False, stop=True)
                STT(so4[c % 4], so4[(c - 1) % 4], kd[:, 0, c, h:h + 1], ps[:, 2 + k2, 128:384], ALU.mult, ALU.add)
                if c % 2 == 1:
                    DMA(st_d[c // 2, 0, h], so4[c % 4], "so%d" % (c % 4))
                if c < NT - 1:
                    ACT(Sfbf[(c + 1) % 2], so4[c % 4], AF.Identity, scale=flags[:, 2, c + 1:c + 2])

            def F2(c):
                k2 = c % 2
                yn = yns[c % 3]
                sc0 = 16 + k2 * 8
                CP(yn, ps[:, 4 + k2, 0:256], eng="act")
                P.op("dve", lambda e, yn=yn, sc0=sc0: e.bn_stats(st8[:, sc0:sc0 + 6], yn), reads=[yn], writes=[st8[:, sc0:sc0 + 6]])
                P.op("dve", lambda e, sc0=sc0: e.bn_aggr(st8[:, sc0 + 6:sc0 + 8], st8[:, sc0:sc0 + 6]),
                     reads=[st8[:, sc0:sc0 + 6]], writes=[st8[:, sc0 + 6:sc0 + 8]])

            def F2b(c):
                k2 = c % 2
                yn = yns[c % 3]
                sc0 = 16 + k2 * 8
                ACT(st8[:, sc0 + 7:sc0 + 8], st8[:, sc0 + 7:sc0 + 8], AF.Sqrt, bias=EPS)
                RECIP(st8[:, sc0 + 7:sc0 + 8], st8[:, sc0 + 7:sc0 + 8])
                TS(yn, yn, st8[:, sc0 + 6:sc0 + 7], st8[:, sc0 + 7:sc0 + 8], ALU.subtract, ALU.mult)
                TT(rts[k2], yn, sg[:, c, :], ALU.mult, eng="pool")

            def F3(c):
                csl = slice(c * 128, (c + 1) * 128)
                k2 = c % 2
                pt = psbf(k2)
                for k in range(2):
                    TR(pt[:, k * 128:(k + 1) * 128], rts[k2][:, k * 128:(k + 1) * 128])
                CP(mixT[:, 2 * h:2 * h + 2, csl], pt[:, 0:256].rearrange("p (k t) -> p k t", k=2), eng="act")

            nxt = p1_items(h + 1) if h + 1 < 4 else []
            stages = [F0, F1, F2, F2b, F3]
            for step in range(NT + 4):
                for s_i, fn in enumerate(stages):
                    c_ = step - s_i
                    if 0 <= c_ < NT:
                        fn(c_)
                if step < len(nxt):
                    nxt[step]()
            for it_ in nxt[NT + 4:]:
                it_()
            if h + 1 < 4:
                TS(kzb.rearrange("p (c n) -> p c n", n=128), kvbufs[(h + 1) % 2][:, :, 0:128], zb[:, h + 1:h + 2], None, ALU.mult)
        for k in range(2):
            mod_mm(1, (6 + k) * 256, 256, A(169984 + k * 4096, [128, 8, 256], BF16), 1, 128 + k * 2)
        mod_fin(1, 0)
        DBG("mixT", mixT[:, :, 0:128], [128, 12, 128])

        build_ggb(0, 192512)
        woe = A(147456, [128, 12, 1024], BF16)
        wst = [A(81920 + k * 4096, [128, 1024], F32) for k in range(2)]
        ffn_prefetch(0)
        for kc in range(12):
            DMA(wst[kc % 2], woe_d[kc * 128:(kc + 1) * 128, :], "woe%d" % (kc % 2))
            TS(woe[:, kc, :], wst[kc % 2], gnpp[:, kc:kc + 1], None, ALU.mult)
        for j in range(NT):
            DMA(X[:, j, :], x_d[j * 128:(j + 1) * 128, :], "xre")
        nst = ffn_norm_stages(0, 0)
        nt_h = min(T, 1024) // 128
        for j in range(NT):
            banks = (4 + (j % 2) * 2, 5 + (j % 2) * 2)
            for hf in range(2):
                for kc in range(12):
                    MM(psb(banks[hf]), mixT[:, kc, j * 128:(j + 1) * 128], woe[:, kc, hf * 512:(hf + 1) * 512],
                       start=(kc == 0), stop=(kc == 11))
            post_norm(j, banks, 0, 90112)
            skewed(nst, j - 1, nt_h)
        for j in range(NT, nt_h + 3):
            skewed(nst, j - 1, nt_h)
        DBG("x1", X[:, 0, :], [128, 1024])
        ffn(0)
        DBG("x2", X[:, 0, :], [128, 1024])

        cur["pp"] = pps[1]
        hT = A(65536, [128, 8, T], BF16)
        wqkv = A(98304, [128, 8, 1536], BF16)
        qTa = A(122880, [128, 8, T], BF16)
        kTa = A(155648, [128, 2, NKC * 128], BF16)
        Vt = A(165888, [128, NKC, 256], BF16)
        qn = A(176128, [128, 10, 128], F32)
        T2 = A(181248, [128, 10, 128], F32)
        qr = A(186368, [128, 10, 128], BF16)
        ropes = [A(188928 + k * 1024, [128, 256], F32) for k in range(2)]
        kvst = [A(190976 + k * 2048, [128, 512], F32) for k in range(2)]
        gains = A(195072, [128, 2, 128], F32)
        ckbf = A(196096, [128, 4, 256], BF16)
        for k3 in range(3):
            DMA(wqkv[:, :, k3 * 512:(k3 + 1) * 512], wqkv_d[:, k3 * 512:(k3 + 1) * 512].rearrange("(c p) f -> p c f", p=128), "wqkv", eng="pool")
        pipeline(NT, [lambda j: norm_A1(X[:, j, :], j, 176128), lambda j: norm_A2(X[:, j, :], j, 176128), lambda j: norm_B(hT, j, 0, 176128)])
        grow = A(181248, [1, 256], F32, 0, 1)
        DMA(grow, qkg_d, "c3")
        MM(ps[:, 7, 0:256], one1[0:1, 0:128], grow[0:1, :])
        CP(gains.rearrange("p a b -> p (a b)"), ps[:, 7, 0:256])
        DMA(ckbf, ck_d.rearrange("(c p) f -> p c f", p=128), "ckv", eng="pool")
        DMA(Vt[:, 0:NPAST, :], cv_d.rearrange("(c p) f -> p c f", p=128), "ckv", eng="pool")
        for c in range(NPAST):
            pt = psbf(2 + c % 2)
            for kh in range(2):
                TR(pt[:, kh * 128:(kh + 1) * 128], ckbf[:, c, kh * 128:(kh + 1) * 128])
            CP(kTa[:, :, c * 128:(c + 1) * 128], pt[:, 0:256].rearrange("p (k t) -> p k t", k=2), eng="act")
        sqj = A(196096, [128, 128], BF16)

        def Q0(j):
            jsl = slice(j * 128, (j + 1) * 128)
            b0 = 2 + (j % 2) * 3
            for k3 in range(3):
                for c in range(8):
                    MM(psb(b0 + k3), hT[:, c, jsl], wqkv[:, c, k3 * 512:(k3 + 1) * 512], start=(c == 0), stop=(c == 7))

        qns = [qn, A(198656, [128, 10, 128], F32)]

        def Qa(j):
            b0 = 2 + (j % 2) * 3
            so_ = 32 + (j % 2) * 16
            for hh in range(10):
                src = ps[:, b0 + hh // 4, (hh % 4) * 128:(hh % 4 + 1) * 128]
                ACT(sqj, src, AF.Square, scale=128.0 ** -0.5, accum=st8[:, so_ + hh:so_ + hh + 1])
            ACT(st8[:, so_:so_ + 10], st8[:, so_:so_ + 10], AF.Sqrt, bias=EPS)
            RECIP(st8[:, so_:so_ + 10], st8[:, so_:so_ + 10])

        def Qb(j):
            qn_ = qns[j % 2]
            jsl = slice(j * 128, (j + 1) * 128)
            b0 = 2 + (j % 2) * 3
            so_ = 32 + (j % 2) * 16
            DMA(ropes[j % 2], rope_d[jsl, :], "rope%d" % (j % 2))
            for hh in range(10):
                src = ps[:, b0 + hh // 4, (hh % 4) * 128:(hh % 4 + 1) * 128]
                STT(qn_[:, hh, :], src, st8[:, so_ + hh:so_ + hh + 1], gains[:, (0 if hh < 8 else 1), :], ALU.mult, ALU.mult)
            ks = kvst[j % 2]
            CP(Vt[:, NPAST + j, :], ps[:, b0 + 2, 256:512])
            CP(ks[:, 256:512], ps[:, b0 + 2, 256:512], eng="act")
            DMA(nk_d[jsl, :].rearrange("p (a b) -> p a b", a=2), qn_[:, 8:10, :], "okv%d" % (j % 2))
            DMA(nv_d[jsl, :], ks[:, 256:512], "okv%d" % (j % 2))

        def Qc(j):
            qn_ = qns[j % 2]
            rp = ropes[j % 2]
            qn5 = qn_.rearrange("p h (a s d) -> p h a s d", a=2, s=2)
            t25 = T2.rearrange("p h (a s d) -> p h a s d", a=2, s=2)
            sn5 = rp[:, 128:256].rearrange("p (a s d) -> p a s d", a=2, s=2)
            for s_ in range(2):
                TT(t25[:, :, :, s_, :], qn5[:, :, :, 1 - s_, :],
                   sn5[:, :, s_, :].unsqueeze(1).to_broadcast([128, 10, 2, 32]), ALU.mult, eng="pool")
            TT(qn_, qn_, rp[:, 0:128].unsqueeze(1).to_broadcast([128, 10, 128]), ALU.mult)
            TT(qr, qn_, T2, ALU.add)

        def Qd(j):
            jsl = slice(j * 128, (j + 1) * 128)
            pt = psbf(0)
            for hh in range(8):
                TR(pt[:, hh * 128:(hh + 1) * 128], qr[:, hh, :])
            pt2 = psbf(1)
            for kh in range(2):
                TR(pt2[:, kh * 128:(kh + 1) * 128], qr[:, 8 + kh, :])
            CP(qTa[:, :, jsl], pt.rearrange("p (h t) -> p h t", h=8), eng="act")
            CP(kTa[:, :, (NPAST + j) * 128:(NPAST + j + 1) * 128], pt2[:, 0:256].rearrange("p (k t) -> p k t", k=2), eng="act")

        for step in range(NT + 4):
            for fn, lag in ((Qd, 4), (Qb, 2), (Qc, 3), (Qa, 1), (Q0, 0)):
                if 0 <= step - lag < NT:
                    fn(step - lag)
        DBG("qT", qTa[:, :, 0:128], [128, 8, 128])
        DBG("kT", kTa[:, :, 0:640], [128, 2, 640])

        oT = A(65536, [128, 8, T], BF16)
        wo = A(98304, [128, 8, 1024], BF16)
        DMA(wo, wo_d.rearrange("(c p) f -> p c f", p=128), "wo", eng="pool")
        NPT = 8
        PTr = [A(176128 + k * 1024, [128, 512], BF16) for k in range(NPT)]
        rinv = [A(198656 + k * 2048, [128, 512], F32) for k in range(2)]
        SC = 128.0 ** -0.5
        steps = [(kvh, pr, sq, kc) for kvh in range(2) for pr in range(2) for sq in range(NSEQ) for kc in range(NKC)]
        NS = len(steps)
        accs = [A(114688 + k * 2048, [128, 512], F32) for k in range(2)]

        def att_S(i):
            kvh, pr, sq, kc = steps[i]
            n0 = kvh * 4 + pr * 2
            MM(psb(i % 3).rearrange("p (h t) -> p h t", h=2), kTa[:, kvh, kc * 128:(kc + 1) * 128],
               qTa[:, n0:n0 + 2, sq * 256:(sq + 1) * 256])

        def att_E(i):
            kvh, pr, sq, kc = steps[i]
            col = kc * NSEQ + sq
            ACT(PTr[i % NPT], psb(i % 3), AF.Exp, scale=SC, bias=maskb[:, col:col + 1])

        def att_V(i):
            kvh, pr, sq, kc = steps[i]
            n0 = kvh * 4 + pr * 2
            it = i // NKC
            bo = 4 + (it % 2) * 2
            bsum = bo + 1
            pt_ = PTr[i % NPT]
            acc = accs[it % 2]
            MM(psb(bo), Vt[:, kc, kvh * 128:(kvh + 1) * 128], pt_, start=(kc == 0), stop=(kc == NKC - 1))
            if kc % 4 != 1:
                MM(psb(bsum), ones_bf[:], pt_, start=(kc == 0), stop=False)
            elif kc == 1:
                CP(acc, pt_)
            else:
                TT(acc, acc, pt_, ALU.add)
            if kc == NKC - 1:
                MM(psb(bsum), onesf[:], acc, start=False, stop=True)
                ri = rinv[it % 2]
                RECIP(ri, psb(bsum))
                TT(oT[:, n0:n0 + 2, sq * 256:(sq + 1) * 256], psb(bo).rearrange("p (h t) -> p h t", h=2),
                   ri.rearrange("p (h t) -> p h t", h=2), ALU.mult)

        LOOK = 2
        for i in range(min(LOOK, NS)):
            att_S(i)
        m2slots = [A(186368 + k * 4096, [128, 8, 256], BF16) for k in range(3)]
        NB2 = 16
        dma_at = {(n_ * NS) // (NB2 + 1): n_ for n_ in range(NB2)}
        assert len(dma_at) == NB2
        for i in range(NS):
            if i + LOOK < NS:
                att_S(i + LOOK)
            att_E(i)
            att_V(i)
            if i in dma_at:
                n_ = dma_at[i]
                if n_ >= 1:
                    mod_mm(1, 2048 + (n_ - 1) * 256, 256, m2slots[(n_ - 1) % 3], 3, 128 + ((n_ - 1) % 3) * 2)
                mod_dma(1, 2048 + n_ * 256, 256, m2slots[n_ % 3], "wa2_%d" % (n_ % 3))
        mod_mm(1, 2048 + (NB2 - 1) * 256, 256, m2slots[(NB2 - 1) % 3], 3, 128 + ((NB2 - 1) % 3) * 2)
        mod_fin(1, 1)
        build_ggb(1, 176128)
        DBG("oT", oT[:, :, 0:128], [128, 8, 128])

        nst = ffn_norm_stages(1, 0)
        ffn_prefetch(1)
        for j in range(NT):
            banks = (4 + (j % 2) * 2, 5 + (j % 2) * 2)
            for hf in range(2):
                for kc in range(8):
                    MM(psb(banks[hf]), oT[:, kc, j * 128:(j + 1) * 128], wo[:, kc, hf * 512:(hf + 1) * 512],
                       start=(kc == 0), stop=(kc == 7))
            post_norm(j, banks, 0, 114688)
            skewed(nst, j - 1, nt_h)
        for j in range(NT, nt_h + 3):
            skewed(nst, j - 1, nt_h)
        DBG("x3", X[:, 0, :], [128, 1024])
        ffn(1)
        for j in range(NT):
            DMA(y_d[j * 128:(j + 1) * 128, :], X[:, j, :], "yout")

        finals = ["yout", "okv0", "okv1", "so0", "so1", "so2", "so3"] + (["dbg"] if dbg and "dbg" in P.dcount else [])
        emit(P, nc, es, final_dma_keys=finals)
    return nc, dbg_outs


def make_consts(NT, sample):
    T = NT * 128
    NKC = NPAST + NT
    NSEQ = NT // 2
    f32 = np.float32
    c = {}
    c["ident"] = np.eye(128, dtype=f32)
    m = np.arange(128)[:, None].astype(f32)
    n = np.arange(128)[None, :].astype(f32)
    BIG = 1.0e6
    EF = np.where(n >= m, n - m, BIG)
    EB = np.where(m >= n, m - n, BIG)
    N1 = np.broadcast_to(n + 1.0, (128, 128))
    N2 = np.broadcast_to(128.0 - n, (128, 128))
    c["retE"] = np.ascontiguousarray(np.stack([EF, EB, N1, N2], axis=1)).astype(f32)
    c["retZ"] = np.stack([127.0 - np.arange(128), np.arange(128)], axis=1).astype(f32)
    wins = (2, 4, 8, 16)
    Base = np.zeros((128, 4, 128), f32)
    Dp = np.zeros((128, 4, 128), f32)
    Dn = np.zeros((128, 4, 128), f32)
    Bp = np.zeros((128, 4, 128), f32)
    Bn = np.zeros((128, 4, 128), f32)
    ccur = np.zeros((4, 128), f32)
    cprev = np.zeros((4, 128), f32)
    cnext = np.zeros((4, 128), f32)
    for g, w in enumerate(wins):
        hlf = w // 2
        for t in range(128):
            for s in range(t - hlf, t + hlf):
                if s < 0:
                    Bp[s + 128, g, t] += 1
                    cprev[g, t] += 1
                elif s >= 128:
                    Bn[s - 128, g, t] += 1
                    cnext[g, t] += 1
                else:
                    Base[s, g, t] += 1
                    ccur[g, t] += 1
            Base[t, g, t] -= ccur[g, t]
            Dp[t, g, t] = -cprev[g, t]
            Dn[t, g, t] = -cnext[g, t]
    c["poolB"] = np.ascontiguousarray(np.stack([Base, Dp, Dn, Bp, Bn], axis=1).reshape(128, 5, 512))
    pc = np.stack([ccur.reshape(512), cprev.reshape(512), cnext.reshape(512)], axis=0)
    c["poolC"] = np.ascontiguousarray(np.broadcast_to(pc[None], (128, 3, 512))).astype(f32)
    j = np.arange(NT)
    if sample:
        hp = (j > 0)
        hn = (j < NT - 1)
        keepf = np.ones(NT)
        keepb = np.ones(NT)
    else:
        hp = (j % 2 == 1)
        hn = (j % 2 == 0)
        keepf = (j % 2 == 1)
        keepb = (j % 2 == 0)
    fl = np.stack([hp, hn, keepf, keepb], axis=0).astype(f32)
    c["flags"] = np.ascontiguousarray(np.broadcast_to(fl[None], (128, 4, NT))).astype(f32)
    mb = np.zeros((NKC, NSEQ), f32)
    if not sample:
        mb[:] = -30000.0
        for kc in range(NPAST, NKC):
            mb[kc, (kc - NPAST) // 2] = 0.0
    c["maskb"] = np.ascontiguousarray(np.broadcast_to(mb.reshape(1, -1), (128, NKC * NSEQ))).astype(f32)
    rope = np.zeros((T, 256), f32)
    if sample:
        t = np.arange(T)
        r = (t // 64).astype(np.float64)
        cl = (t % 64).astype(np.float64)
        freqs = 10000.0 ** (-np.arange(32, dtype=np.float64) / 32.0)
        for a, pos in enumerate((r, cl)):
            ang = pos[:, None] * freqs[None, :]
            cs = np.cos(ang).astype(f32)
            sn = np.sin(ang).astype(f32)
            rope[:, a * 64:a * 64 + 32] = cs
            rope[:, a * 64 + 32:a * 64 + 64] = cs
            rope[:, 128 + a * 64:128 + a * 64 + 32] = -sn
            rope[:, 128 + a * 64 + 32:128 + a * 64 + 64] = sn
    else:
        rope[:, 0:128] = 1.0
    c["rope"] = rope
    return c


_NC_CACHE = {}


def run_cores(NT, core_inputs, dbg=False, runner=None):
    key = (NT, dbg)
    if key not in _NC_CACHE:
        _NC_CACHE[key] = build(NT, dbg)
    nc, dbg_outs = _NC_CACHE[key]
    if runner is not None:
        return runner(nc, core_inputs)
    res = run_bass_kernel_spmd(nc, core_inputs, core_ids=list(range(len(core_inputs))))
    return res.results


def prep_core(NT, sample, xs, ccv, s0, ck, cv, W):
    f32 = np.float32
    m = dict(W)
    m.update(make_consts(NT, sample))
    m["x"] = np.ascontiguousarray(xs, dtype=f32)
    m["cc"] = np.ascontiguousarray(ccv.reshape(128, 8), dtype=f32)
    m["s0"] = np.ascontiguousarray(s0, dtype=f32)
    m["ck"] = np.ascontiguousarray(ck.reshape(512, 256), dtype=f32)
    m["cv"] = np.ascontiguousarray(cv.reshape(512, 256), dtype=f32)
    return m


def prep_weights(w_ada, b_ada, norm_gain, w_ffn_in, w_ffn_out, w_in_even, ret_decay_logit, ret_gn_gain, pool_w,
                 pool_scale, w_out_even, w_qkv, q_norm_gain, k_norm_gain, w_o):
    f32 = np.float32
    ca = lambda a: np.ascontiguousarray(a, dtype=f32)
    return {
        "w_ada": ca(w_ada), "b_ada": ca(b_ada), "norm_gain": ca(norm_gain.reshape(2, 4 * D)),
        "w_ffn_in": ca(w_ffn_in), "w_ffn_out": ca(w_ffn_out), "w_in_even": ca(w_in_even[0]),
        "dlog": ca(ret_decay_logit.reshape(1, 8)),
        "gn": ca(ret_gn_gain[0].reshape(8, 128).T),
        "pool_w": ca(pool_w[0]), "pscale": ca(pool_scale[0].reshape(4, 128).T),
        "w_out_even": ca(w_out_even[0]), "w_qkv": ca(w_qkv[0]),
        "qkg": ca(np.concatenate([q_norm_gain[0], k_norm_gain[0]]).reshape(1, 256)),
        "w_o": ca(w_o[0]),
    }


def kernel(x_prompt, x_sample, state_ret, cache_k, cache_v, c, c_ctx, w_ada, b_ada, norm_gain,
           w_ffn_in, w_ffn_out, w_in_even, ret_decay_logit, ret_gn_gain, pool_w, pool_scale,
           w_out_even, w_qkv, q_norm_gain, k_norm_gain, w_o, _runner=None, _dbg=False):
    x_prompt = np.asarray(x_prompt)
    x_sample = np.asarray(x_sample)
    B, S, _ = x_prompt.shape
    DB, DS, _ = x_sample.shape
    NT = DS // 128
    npc = (B * S) // (NT * 128)
    spc = B // npc
    W = prep_weights(*[np.asarray(a) for a in (w_ada, b_ada, norm_gain, w_ffn_in, w_ffn_out, w_in_even, ret_decay_logit,
                                               ret_gn_gain, pool_w, pool_scale, w_out_even, w_qkv, q_norm_gain,
                                               k_norm_gain, w_o)])
    f32 = np.float32
    cores = []
    zs = np.zeros((2, 4, 128, 256), f32)
    zk = np.zeros((512, 256), f32)
    for i in range(npc):
        xs = x_prompt[i * spc:(i + 1) * spc].reshape(NT * 128, D)
        cores.append(prep_core(NT, False, xs, np.asarray(c_ctx), zs, zk, zk, W))
    state_ret = np.asarray(state_ret)
    cache_k = np.asarray(cache_k)
    cache_v = np.asarray(cache_v)
    for b in range(DB):
        cores.append(prep_core(NT, True, x_sample[b], np.asarray(c)[b], state_ret[b, 0], cache_k[b, 0], cache_v[b, 0], W))
    res = run_cores(NT, cores, dbg=_dbg, runner=_runner)
    y_p = np.concatenate([res[i]["y"].reshape(spc, S, D) for i in range(npc)], axis=0)
    y_s = np.stack([res[npc + b]["y"] for b in range(DB)], axis=0)
    st = np.concatenate([res[i]["st"] for i in range(npc)], axis=0)[:, None]
    nk = np.concatenate([res[i]["nk"].reshape(spc, S, 2, 128) for i in range(npc)], axis=0)[:, None]
    nv = np.concatenate([res[i]["nv"].reshape(spc, S, 2, 128) for i in range(npc)], axis=0)[:, None]
    out = (y_p.astype(f32), y_s.astype(f32), st.astype(f32), nk.astype(f32), nv.astype(f32))
    if _dbg:
        return out, res
    return out
```

```python
import contextlib
import numpy as np
import concourse.bass as bass
import concourse.mybir as mybir
from concourse.bass_utils import run_bass_kernel_spmd

dt = mybir.dt
AF = mybir.ActivationFunctionType
ALU = mybir.AluOpType
AX = mybir.AxisListType
F32 = dt.float32
BF16 = dt.bfloat16
U8 = dt.uint8
ESZ = {dt.float32: 4, dt.bfloat16: 2, dt.uint8: 1, dt.int32: 4, dt.float16: 2}

D = 1024
DFF = 2816
NFC = 22
EPS = 1e-6
NPAST = 4


def _region(ap):
    a = ap.ap
    row, npart = a[0]
    off = ap.offset
    if row:
        p0, c0 = divmod(off, row)
    else:
        p0, c0 = 0, off
    ext = 1
    for st, cnt in a[1:]:
        ext += (cnt - 1) * abs(st)
    es = ESZ[ap.dtype]
    return (ap.tensor.name, p0, p0 + npart, c0 * es, (c0 + ext) * es)


class Op:
    __slots__ = ("eng", "fn", "deps", "sig", "tick", "dkey", "dcnt", "waits", "dwaits", "dsnap")

    def __init__(self, eng, fn, dkey=None):
        self.eng = eng
        self.fn = fn
        self.deps = set()
        self.sig = False
        self.tick = 0
        self.dkey = dkey
        self.dcnt = 0
        self.waits = None
        self.dwaits = None


class Prog:
    ENGS = ("pe", "act", "dve", "pool", "sp")

    def __init__(self):
        self.ops = []
        self.recs = {}
        self.dcount = {}

    def op(self, eng, fn, reads=(), writes=(), dkey=None):
        o = Op(eng, fn, dkey)
        if dkey is not None:
            self.dcount[dkey] = self.dcount.get(dkey, 0) + 1
            o.dcnt = self.dcount[dkey]
        for ap in reads:
            self._access(o, ap, "R")
        for ap in writes:
            self._access(o, ap, "W")
        o.dsnap = {d.dkey: self.dcount[d.dkey] - (1 if d.dkey == dkey else 0) for d in o.deps if d.dkey is not None}
        self.ops.append(o)
        return o

    def _access(self, o, ap, kind):
        if ap is None or isinstance(ap, (int, float)):
            return
        sp = str(ap.space)
        if "SB" not in sp and "PSUM" not in sp:
            return
        name, plo, phi, lo, hi = _region(ap)
        ispsum = "PSUM" in sp
        if ispsum:
            lo = (lo // 2048) * 2048
            hi = ((hi + 2047) // 2048) * 2048
            plo, phi = 0, 128
        L = self.recs.get(name)
        if L is None:
            L = self.recs[name] = []
        newL = []
        isdma = o.dkey is not None
        for r in L:
            rplo, rphi, rlo, rhi, rkind, rop = r
            if rlo < hi and lo < rhi and rplo < phi and plo < rphi:
                if kind == "W" or rkind == "W" or (ispsum and rop.eng != o.eng):
                    if rop is not o:
                        o.deps.add(rop)
                    if kind == "W" and lo <= rlo and rhi <= hi and plo <= rplo and rphi <= phi:
                        continue
                    if kind == "R" and rkind == "W" and rop.dkey is not None and not isdma:
                        newL.append((rplo, rphi, rlo, rhi, "W", o))
                        continue
                elif (not isdma) and rop.dkey is None and rop.eng == o.eng and rlo == lo and rhi == hi \
                        and rplo == plo and rphi == phi:
                    continue
            newL.append(r)
        newL.append((plo, phi, lo, hi, kind, o))
        self.recs[name] = newL

    def finalize(self):
        for o in self.ops:
            for d in o.deps:
                if d.dkey is None:
                    if d.eng == "pe" and o.eng == "pe" and o.dkey is None:
                        continue
                    d.sig = True
        cnt = {e: 0 for e in self.ENGS}
        for o in self.ops:
            if o.dkey is None and o.sig:
                cnt[o.eng] += 1
                o.tick = cnt[o.eng]
        known = {e: {x: 0 for x in self.ENGS} for e in self.ENGS}
        dknown = {e: {} for e in self.ENGS}
        tick_clock = {e: {} for e in self.ENGS}
        for o in self.ops:
            need = {}
            dneed = {}
            for d in o.deps:
                if d.dkey is not None:
                    dneed[d.dkey] = max(dneed.get(d.dkey, 0), o.dsnap[d.dkey])
                else:
                    if d.eng == "pe" and o.eng == "pe" and o.dkey is None:
                        continue
                    need[d.eng] = max(need.get(d.eng, 0), d.tick)
            kn = known[o.eng]
            waits = []
            for e, t in need.items():
                if kn[e] < t:
                    waits.append((e, t))
                    kn[e] = t
                    ck = tick_clock[e].get(t)
                    if ck:
                        for x, v in ck.items():
                            if kn[x] < v:
                                kn[x] = v
            dwaits = []
            dk = dknown[o.eng]
            for k, c in dneed.items():
                if dk.get(k, 0) < c:
                    dwaits.append((k, c))
                    dk[k] = c
            o.waits = waits
            o.dwaits = dwaits
            if o.dkey is None and o.sig:
                tick_clock[o.eng][o.tick] = dict(kn)


def emit(prog, nc, es, final_dma_keys=()):
    engs = Prog.ENGS
    sems = {e: es.enter_context(nc.semaphore("tick_" + e)) for e in engs if e != "sp"}
    dsems = {k: es.enter_context(nc.semaphore("d_" + k)) for k in prog.dcount}
    prog.finalize()
    block = es.enter_context(nc.Block())
    per = {e: [o for o in prog.ops if o.eng == e] for e in engs}

    def run(e, eng):
        for o in per[e]:
            for (we, t) in o.waits:
                eng.wait_ge(sems[we], t)
            for (k, c) in o.dwaits:
                eng.wait_ge(dsems[k], 16 * c)
            ins = o.fn(eng)
            if o.dkey is not None:
                ins.then_inc(dsems[o.dkey], 16)
            elif o.sig:
                ins.then_inc(sems[e], 1)
        if e == "sp":
            for k in final_dma_keys:
                eng.wait_ge(dsems[k], 16 * prog.dcount[k])

    @block.tensor
    def _(eng):
        run("pe", eng)

    @block.scalar
    def _(eng):
        run("act", eng)

    @block.vector
    def _(eng):
        run("dve", eng)

    @block.gpsimd
    def _(eng):
        run("pool", eng)

    @block.sync
    def _(eng):
        run("sp", eng)


def build(NT, dbg=False):
    T = NT * 128
    NKC = NPAST + NT
    NSEQ = NT // 2
    NQB = T // 512
    assert T % 1024 == 0 or T == 512
    nc = bass.Bass("TRN2", target_bir_lowering=False)
    es = contextlib.ExitStack()
    P = Prog()

    def din(name, shape):
        return nc.dram_tensor(name, list(shape), F32, kind="ExternalInput").ap()

    def dout(name, shape):
        return nc.dram_tensor(name, list(shape), F32, kind="ExternalOutput").ap()

    x_d = din("x", [T, D])
    cc_d = din("cc", [128, 8])
    s0_d = din("s0", [2, 4, 128, 256])
    ck_d = din("ck", [512, 256])
    cv_d = din("cv", [512, 256])
    wada_d = din("w_ada", [2, D, 6 * D])
    bada_d = din("b_ada", [2, 6 * D])
    ngain_d = din("norm_gain", [2, 4 * D])
    wfi_d = din("w_ffn_in", [2, D, 2 * DFF])
    wfo_d = din("w_ffn_out", [2, DFF, D])
    win_d = din("w_in_even", [D, 3584])
    dlog_d = din("dlog", [1, 8])
    gn_d = din("gn", [128, 8])
    poolw_d = din("pool_w", [4, 128, 128])
    pscale_d = din("pscale", [128, 4])
    woe_d = din("w_out_even", [1536, D])
    wqkv_d = din("w_qkv", [D, 1536])
    qkg_d = din("qkg", [1, 256])
    wo_d = din("w_o", [D, D])
    ident_d = din("ident", [128, 128])
    retE_d = din("retE", [128, 4, 128])
    retZ_d = din("retZ", [128, 2])
    poolB_d = din("poolB", [128, 5, 512])
    poolC_d = din("poolC", [128, 3, 512])
    flags_d = din("flags", [128, 4, NT])
    maskb_d = din("maskb", [128, NKC * NSEQ])
    rope_d = din("rope", [T, 256])

    y_d = dout("y", [T, D])
    st_d = dout("st", [NSEQ, 2, 4, 128, 256])
    nk_d = dout("nk", [T, 256])
    nv_d = dout("nv", [T, 256])
    dbg_outs = {}

    with es:
        ARENA = 198656 + 8192
        arena = es.enter_context(nc.sbuf_tensor("arena", [128, ARENA], U8))
        ps = es.enter_context(nc.psum_tensor("ps", [128, 8, 512], F32))

        def sbt(name, shape, dtype):
            return es.enter_context(nc.sbuf_tensor(name, list(shape), dtype))

        def A(off, shape, dtype, p0=0, p1=128):
            n = 1
            for s in shape[1:]:
                n *= s
            nb = n * ESZ[dtype]
            assert off % 4 == 0 and off + nb <= ARENA, (off, nb)
            v = arena[p0:p1, off:off + nb].bitcast(dtype)
            if len(shape) == 3:
                v = v.rearrange("p (a b) -> p a b", a=shape[1])
            elif len(shape) == 4:
                v = v.rearrange("p (a b c) -> p a b c", a=shape[1], b=shape[2])
            return v

        ident = sbt("ident_sb", [128, 128], BF16)
        ones_bf = sbt("ones_bf", [128, 128], BF16)
        one1 = sbt("one1", [1, 128], F32)
        pps = [sbt("pp_sb%d" % i_, [128, 4, 8], F32) for i_ in range(2)]
        ggcs = [sbt("ggc_sb%d" % i_, [128, 2, 8], F32) for i_ in range(2)]
        cur = {"pp": pps[0]}
        identf = sbt("identf", [128, 128], F32)
        onesf = sbt("onesf", [128, 128], F32)
        modT = sbt("modT", [128, 2, 48], F32)
        bnT = sbt("bnT", [128, 2, 80], F32)
        s_bf = sbt("s_bf", [128, 8], BF16)
        st8 = sbt("st8", [128, 64], F32)
        flags = sbt("flags_sb", [128, 4, NT], F32)
        maskb = sbt("maskb_sb", [128, NKC * NSEQ], F32)
        gnpp = sbt("gnpp", [128, 12], F32)
        ggb = A(198656, [128, 2, 1024], F32)
        X = A(0, [128, NT, 1024], F32)

        def psb(b):
            return ps[:, b, :]

        def psbf(b):
            return ps[:, b, :].bitcast(BF16)

        def MM(out, lhsT, rhs, start=True, stop=True):
            P.op("pe", lambda e: e.matmul(out, lhsT, rhs, start=start, stop=stop), reads=[lhsT, rhs], writes=[out])

        def TR(out, in_):
            P.op("pe", lambda e: e.transpose(out, in_, ident[:]), reads=[in_, ident[:]], writes=[out])

        def ACT(out, in_, func, bias=0.0, scale=1.0, accum=None):
            rd = [in_] + [a for a in (bias, scale) if not isinstance(a, (int, float))]
            wr = [out] + ([accum] if accum is not None else [])
            if accum is not None:
                P.op("act", lambda e: e.activation(out, in_, func, bias=bias, scale=scale, accum_out=accum), reads=rd, writes=wr)
            else:
                P.op("act", lambda e: e.activation(out, in_, func, bias=bias, scale=scale), reads=rd, writes=wr)

        def TT(out, a, b, op, eng="dve"):
            P.op(eng, lambda e: e.tensor_tensor(out, a, b, op), reads=[a, b], writes=[out])

        def TS(out, a, s1, s2, op0, op1=None, eng="dve"):
            rd = [a] + [s for s in (s1, s2) if s is not None and not isinstance(s, (int, float))]
            if op1 is None:
                P.op(eng, lambda e: e.tensor_scalar(out, a, s1, None, op0), reads=rd, writes=[out])
            else:
                P.op(eng, lambda e: e.tensor_scalar(out, a, s1, s2, op0, op1), reads=rd, writes=[out])

        def STT(out, a, s, b, op0, op1, eng="dve"):
            rd = [a, b] + ([] if isinstance(s, (int, float)) else [s])
            P.op(eng, lambda e: e.scalar_tensor_tensor(out, a, s, b, op0, op1), reads=rd, writes=[out])

        def CP(out, in_, eng="dve"):
            if eng == "act":
                P.op("act", lambda e: e.copy(out, in_), reads=[in_], writes=[out])
            else:
                P.op(eng, lambda e: e.tensor_copy(out, in_), reads=[in_], writes=[out])

        def RECIP(out, in_):
            P.op("dve", lambda e: e.reciprocal(out, in_), reads=[in_], writes=[out])

        def MEMSET(ap, v, eng="dve"):
            P.op(eng, lambda e: e.memset(ap, v), writes=[ap])

        def DMA(out, in_, key, eng="sp"):
            P.op(eng, lambda e: e.dma_start(out=out, in_=in_), reads=[in_], writes=[out], dkey=key)

        dbg_n = [0]

        def DBG(name, ap, shape):
            if not dbg:
                return
            assert NT <= 8
            o = dout("dbg_" + name, shape)
            dbg_outs[name] = o
            if ap.dtype != F32:
                tmp = A(32768, list(shape), F32)
                CP(tmp, ap)
                DMA(o, tmp, "dbg")
            else:
                DMA(o, ap, "dbg")

        DMA(ident[:], ident_d, "c0", eng="pool")
        MEMSET(ones_bf[:], 1.0)
        MEMSET(one1[:], 1.0)
        MEMSET(onesf[:], 1.0)
        DMA(identf[:], ident_d, "c1")
        DMA(flags[:], flags_d, "c1")
        DMA(maskb[:], maskb_d, "c1")
        DMA(gnpp[:, 0:8], gn_d, "c1")
        DMA(gnpp[:, 8:12], pscale_d, "c1")

        MS = 186368
        ccs = A(MS + 8192, [128, 8], F32)
        b48 = A(MS + 8448, [48, 128], F32, 0, 48)
        n32 = A(MS + 8960, [32, 128], F32, 0, 32)
        mod_state = {"nblk": 0, "inflight": None}

        def mod_init():
            DMA(ccs, cc_d, "m0")
            ACT(s_bf[:], ccs, AF.Silu)
            for i in range(2):
                DMA(b48, bada_d[i:i + 1, :].rearrange("o (c p) -> (o c) p", p=128), "mb")
                DMA(n32, ngain_d[i:i + 1, :].rearrange("o (c p) -> (o c) p", p=128), "mb")
                MM(ps[:, 7, 0:48], b48, identf[0:48, 0:48])
                MM(ps[:, 7, 64:96], n32, identf[0:32, 0:32])
                CP(bnT[:, i, 0:48], ps[:, 7, 0:48])
                CP(bnT[:, i, 48:80], ps[:, 7, 64:96])

        def mod_dma(i, col0, width, slot, key):
            DMA(slot, wada_d[i][:, col0:col0 + width].rearrange("(p c) f -> p c f", c=8), key, eng="pool")

        def mod_mm(i, col0, width, slot, bank, pc0):
            nf = width // 128
            f0 = col0 // 128
            for fc in range(nf):
                for c in range(8):
                    MM(ps[:, bank, pc0 + fc:pc0 + fc + 1], slot[:, c, fc * 128:(fc + 1) * 128], s_bf[:, c:c + 1],
                       start=(c == 0), stop=(c == 7))
            TT(modT[:, i, f0:f0 + nf], ps[:, bank, pc0:pc0 + nf], bnT[:, i, f0:f0 + nf], ALU.add)

        def mod_fin(i, part):
            pp_ = pps[i]
            m = modT[:, i, :]
            g = bnT[:, i, 48:80]
            if part == 0:
                STT(pp_[:, 0, :], m[:, 8:16], 1.0, g[:, 0:8], ALU.add, ALU.mult)
                CP(pp_[:, 1, :], m[:, 0:8])
            else:
                STT(pp_[:, 2, :], m[:, 32:40], 1.0, g[:, 16:24], ALU.add, ALU.mult)
                CP(pp_[:, 3, :], m[:, 24:32])
                TT(ggcs[i][:, 0, :], m[:, 16:24], g[:, 8:16], ALU.mult)
                TT(ggcs[i][:, 1, :], m[:, 40:48], g[:, 24:32], ALU.mult)

        def build_ggb(i, tmp_off):
            for v in range(2):
                for c in range(8):
                    dg = A(tmp_off + ((v * 8 + c) % 2) * 512, [128, 128], F32)
                    TS(dg, identf[:], ggcs[i][:, v, c:c + 1], None, ALU.mult)
                    b = 2 + (c // 4) + v * 2
                    MM(ps[:, b, (c % 4) * 128:(c % 4 + 1) * 128], onesf[:], dg)
                for hf in range(2):
                    CP(ggb[:, v, hf * 512:(hf + 1) * 512], psb(2 + hf + v * 2), eng="act")

        def pipeline(n, stages, oldest_first=False):
            ns = len(stages)
            for step in range(n + ns - 1):
                order = list(enumerate(stages))
                if oldest_first:
                    order = order[::-1]
                for s_i, fn in order:
                    it_ = step - s_i
                    if 0 <= it_ < n:
                        fn(it_)

        def norm_A1(src_tile, j, tmp_off):
            junk = A(tmp_off + 4096, [128, 1024], BF16)
            ms = st8[:, (j % 2) * 2:(j % 2) * 2 + 1]
            ACT(junk, src_tile, AF.Square, scale=1.0 / 32.0, accum=ms)

        def norm_A2(src_tile, j, tmp_off):
            xn = A(tmp_off + (j % 2) * 2048, [128, 1024], BF16)
            ms = st8[:, (j % 2) * 2:(j % 2) * 2 + 1]
            rs = st8[:, (j % 2) * 2 + 1:(j % 2) * 2 + 2]
            ACT(rs, ms, AF.Sqrt, bias=EPS)
            RECIP(rs, rs)
            TS(xn, src_tile, rs, None, ALU.mult)

        def skewed(stages, jj, n):
            for k, fn in enumerate(stages):
                if 0 <= jj - k < n:
                    fn(jj - k)

        def norm_B(hT_dst, j, vsel, tmp_off):
            xn = A(tmp_off + (j % 2) * 2048, [128, 1024], BF16)
            pts = (psbf((j % 2) * 2), psbf((j % 2) * 2 + 1))
            for c in range(8):
                TR(pts[c % 2][:, (c // 2) * 128:(c // 2 + 1) * 128], xn[:, c * 128:(c + 1) * 128])
            for c in range(8):
                src = pts[c % 2][:, (c // 2) * 128:(c // 2 + 1) * 128]
                if c % 2 == 0:
                    ACT(hT_dst[:, c, j * 128:(j + 1) * 128], src, AF.Identity,
                        scale=cur['pp'][:, vsel, c:c + 1], bias=cur['pp'][:, vsel + 1, c:c + 1])
                else:
                    TS(hT_dst[:, c, j * 128:(j + 1) * 128], src,
                       cur['pp'][:, vsel, c:c + 1], cur['pp'][:, vsel + 1, c:c + 1], ALU.mult, ALU.add)

        def post_norm(j, banks, v, tmp_off):
            tmp = A(tmp_off, [128, 1024], F32)
            junk = A(tmp_off + 4096 - 2048, [128, 512], F32) if False else None
            c0 = 8 + (j % 2) * 4
            for hf in range(2):
                ACT(tmp[:, hf * 512:(hf + 1) * 512], psb(banks[hf]), AF.Square, scale=1.0 / 32.0, accum=st8[:, c0 + hf:c0 + hf + 1])
            TT(st8[:, c0 + 2:c0 + 3], st8[:, c0:c0 + 1], st8[:, c0 + 1:c0 + 2], ALU.add)
            ACT(st8[:, c0 + 3:c0 + 4], st8[:, c0 + 2:c0 + 3], AF.Sqrt, bias=EPS)
            RECIP(st8[:, c0 + 3:c0 + 4], st8[:, c0 + 3:c0 + 4])
            for hf in range(2):
                sl = slice(hf * 512, (hf + 1) * 512)
                STT(tmp[:, sl], psb(banks[hf]), st8[:, c0 + 3:c0 + 4], ggb[:, v, sl], ALU.mult, ALU.mult)
                TT(X[:, j, sl], X[:, j, sl], tmp[:, sl], ALU.add, eng="pool")

        def ffn_layout(i):
            if i == 0:
                return dict(h2T=65536, hidT=81920, wout=126976, wins=172032, sil=188416, TMP=192512)
            return dict(h2T=155648, hidT=65536, wout=110592, wins=172032, sil=188416, TMP=192512)

        def ffn_norm_stages(i, th):
            L = ffn_layout(i)
            HT = min(T, 1024)
            ntile = HT // 128
            h2T = A(L["h2T"], [128, 8, HT], BF16)
            return [lambda jj: norm_A1(X[:, th * ntile + jj, :], jj, L["TMP"]),
                    lambda jj: norm_A2(X[:, th * ntile + jj, :], jj, L["TMP"]),
                    lambda jj: norm_B(h2T, jj, 2, L["TMP"])]

        ffn_pref = {}

        def ffn_win_dma(i, fg):
            L = ffn_layout(i)
            sl = A(L["wins"] + (fg % 2) * 8192, [128, 8, 2, 256], BF16)
            for part in range(2):
                c0 = part * DFF + fg * 256
                DMA(sl[:, :, part, :], wfi_d[i][:, c0:c0 + 256].rearrange("(c p) f -> p c f", p=128), "wfi%d" % (fg % 2), eng="pool")

        def ffn_prefetch(i):
            for fg in range(2):
                ffn_win_dma(i, fg)
            ffn_pref[i] = True

        def ffn(i):
            L = ffn_layout(i)
            HT = min(T, 1024)
            NH = T // HT
            ntile = HT // 128
            h2T = A(L["h2T"], [128, 8, HT], BF16)
            hidT = A(L["hidT"], [128, NFC, HT], BF16)
            wout = A(L["wout"], [128, NFC, 1024], BF16)
            wins = [A(L["wins"] + k * 8192, [128, 8, 2, 256], BF16) for k in range(2)]
            sil = [A(L["sil"] + k * 2048, [128, 512], F32) for k in range(2)]
            for th in range(NH):
                for fg in range(NFC // 2):
                    sl = wins[fg % 2]
                    if not (th == 0 and fg < 2 and ffn_pref.get(i)):
                        ffn_win_dma(i, fg)
                    if th == 0:
                        f0 = fg * 2
                        DMA(wout[:, f0:f0 + 2, :], wfo_d[i][f0 * 128:(f0 + 2) * 128, :].rearrange("(c p) n -> p c n", p=128), "wfo", eng="pool")
                    for fi in range(2):
                        f = fg * 2 + fi
                        for tb in range(HT // 512):
                            ba = 4 + ((f * 2 + tb) % 2) * 2
                            bb = ba + 1
                            tsl = slice(tb * 512, (tb + 1) * 512)
                            for c in range(8):
                                MM(psb(ba), sl[:, c, 0, fi * 128:(fi + 1) * 128], h2T[:, c, tsl], start=(c == 0), stop=(c == 7))
                            for c in range(8):
                                MM(psb(bb), sl[:, c, 1, fi * 128:(fi + 1) * 128], h2T[:, c, tsl], start=(c == 0), stop=(c == 7))
                            s_ = sil[(f * 2 + tb) % 2]
                            ACT(s_, psb(ba), AF.Silu)
                            TT(hidT[:, f, tsl], s_, psb(bb), ALU.mult)
                nst = ffn_norm_stages(i, th + 1) if th + 1 < NH else None
                for jj in range(ntile + 2):
                    if jj < ntile:
                        j = th * ntile + jj
                        banks = (4 + (jj % 2) * 2, 5 + (jj % 2) * 2)
                        for hf in range(2):
                            for f in range(NFC):
                                MM(psb(banks[hf]), hidT[:, f, jj * 128:(jj + 1) * 128], wout[:, f, hf * 512:(hf + 1) * 512],
                                   start=(f == 0), stop=(f == NFC - 1))
                        post_norm(j, banks, 1, L["sil"])
                    if nst is not None:
                        skewed(nst, jj, ntile)

        mod_init()
        m0slots = [A(k * 8192, [128, 8, 512], BF16) for k in range(4)]
        for cb in range(4):
            mod_dma(0, cb * 512, 512, m0slots[cb], "wa0_%d" % cb)
        for cb in range(4):
            mod_mm(0, cb * 512, 512, m0slots[cb], 7, 128 + cb * 4)
        mod_fin(0, 0)
        p0slots = [A(o_, [128, 8, 512], BF16) for o_ in (49152, 57344, 160768, 168960, 186368)]

        def mod0_rest(j):
            n_dma = j
            if 0 <= n_dma < 8:
                mod_dma(0, (4 + n_dma) * 512, 512, p0slots[n_dma % 5], "wa0_%d" % (4 + n_dma % 5))
            n_mm = j - 4
            if 0 <= n_mm < 8:
                mod_mm(0, (4 + n_mm) * 512, 512, p0slots[n_mm % 5], 7, 128 + (n_mm % 5) * 4)

        hT = A(65536, [128, 8, T], BF16)
        mixT = A(98304, [128, 12, T], BF16)
        LT = 160768
        NXI = 6
        xin = [A(32768 + k * 4096, [128, 1024], F32) for k in range(NXI)]
        def l0_A(j):
            DMA(xin[j % NXI], x_d[j * 128:(j + 1) * 128, :], "xin%d" % (j % NXI))
            norm_A1(xin[j % NXI], j, LT + 8192)
        def l0_B(j):
            norm_B(hT, j, 0, LT + 8192)
        pipeline(NT, [l0_A, lambda j: norm_A2(xin[j % NXI], j, LT + 8192), l0_B])
        DBG("hT0", hT[:, :, 0:128], [128, 8, 128])

        RC = LT + 16384
        lg = A(RC, [128, 8], F32)
        dlb = A(RC + 32, [128, 8], F32)
        dec = A(RC + 64, [128, 8], F32)
        retE = A(RC + 128, [128, 4, 128], F32)
        retZ = A(RC + 128 + 2048, [128, 2], F32)
        Mall = A(RC + 2304, [128, 4, 128], BF16)
        xif = A(RC + 3328, [128, 4, 128], F32)
        xib = A(RC + 5376, [128, 4, 128], F32)
        zf = A(RC + 7424, [128, 4], F32)
        zb = A(RC + 7440, [128, 4], F32)
        mtmp = A(RC + 7456, [128, 2, 128], F32)
        kd = A(RC + 8480, [128, 2, NT, 4], F32)
        DMA(dlb, dlog_d.partition_broadcast(128), "c2")
        DMA(retE, retE_d, "c2")
        DMA(retZ, retZ_d, "c2")
        ACT(lg, dlb, AF.Exp, scale=-1.0)
        ACT(lg, lg, AF.Ln, bias=1.0)
        TS(lg, lg, -1.0, None, ALU.mult)
        ACT(dec, lg, AF.Exp, scale=128.0)
        SQ = 128.0 ** -0.5
        for h in range(4):
            ACT(mtmp[:, 0, :], retE[:, 0, :], AF.Exp, scale=lg[:, h:h + 1])
            ACT(mtmp[:, 1, :], retE[:, 1, :], AF.Exp, scale=lg[:, 4 + h:5 + h])
            TT(mtmp[:, 0, :], mtmp[:, 0, :], mtmp[:, 1, :], ALU.add)
            TS(Mall[:, h, :], mtmp[:, 0, :], SQ, None, ALU.mult)
            ACT(xif[:, h, :], retE[:, 2, :], AF.Exp, scale=lg[:, h:h + 1])
            ACT(xib[:, h, :], retE[:, 3, :], AF.Exp, scale=lg[:, 4 + h:5 + h])
            ACT(zf[:, h:h + 1], retZ[:, 0:1], AF.Exp, scale=lg[:, h:h + 1])
            ACT(zb[:, h:h + 1], retZ[:, 1:2], AF.Exp, scale=lg[:, 4 + h:5 + h])
        TS(xif.rearrange("p a b -> p (a b)"), xif.rearrange("p a b -> p (a b)"), SQ, None, ALU.mult)
        TS(xib.rearrange("p a b -> p (a b)"), xib.rearrange("p a b -> p (a b)"), SQ, None, ALU.mult)
        for d_ in range(2):
            for c in range(NT):
                TS(kd[:, d_, c, :], dec[:, d_ * 4:d_ * 4 + 4], flags[:, 2 + d_, c:c + 1], None, ALU.mult)

        u_tok = A(0, [128, NT, 512], BF16)
        Wu = A(16384, [128, 8, 512], BF16)
        pB = A(24576, [128, 5, 512], BF16)
        pC = A(29696, [128, 3, 512], F32)
        pw = A(35840, [128, 4, 128], BF16)
        DMA(Wu, win_d[:, 3072:3584].rearrange("(c p) f -> p c f", p=128), "wu", eng="pool")
        DMA(pB, poolB_d, "wu", eng="pool")
        DMA(pC, poolC_d, "c4")
        DMA(pw, poolw_d.rearrange("g c d -> c g d"), "wu", eng="pool")
        for j in range(NT):
            b = 2 + j % 2
            for c in range(8):
                MM(psb(b), hT[:, c, j * 128:(j + 1) * 128], Wu[:, c, :], start=(c == 0), stop=(c == 7))
            CP(u_tok[:, j, :], psb(b), eng="act")
        PO = 36864
        for j in range(NT):
            k2 = j % 2
            Cc = A(PO + k2 * 3072, [128, 512], BF16)
            Cp = A(PO + k2 * 3072 + 1024, [128, 512], BF16)
            Cn = A(PO + k2 * 3072 + 2048, [128, 512], BF16)
            cnt = A(PO + 6144 + k2 * 2048, [128, 512], F32)
            dT = A(PO + 10240 + k2 * 1024, [128, 512], BF16)
            hp = flags[:, 0, j:j + 1]
            hn = flags[:, 1, j:j + 1]
            STT(Cc, pB[:, 1, :], hp, pB[:, 0, :], ALU.mult, ALU.add)
            STT(Cc, pB[:, 2, :], hn, Cc, ALU.mult, ALU.add)
            TS(Cp, pB[:, 3, :], hp, None, ALU.mult)
            TS(Cn, pB[:, 4, :], hn, None, ALU.mult)
            STT(cnt, pC[:, 1, :], hp, pC[:, 0, :], ALU.mult, ALU.add)
            STT(cnt, pC[:, 2, :], hn, cnt, ALU.mult, ALU.add)
            RECIP(cnt, cnt)
            jp = max(j - 1, 0)
            jn = min(j + 1, NT - 1)
            b = 4 + k2
            for g in range(4):
                gs = slice(g * 128, (g + 1) * 128)
                MM(ps[:, b, gs], u_tok[:, jp, gs], Cp[:, gs], start=True, stop=False)
                MM(ps[:, b, gs], u_tok[:, j, gs], Cc[:, gs], start=False, stop=False)
                MM(ps[:, b, gs], u_tok[:, jn, gs], Cn[:, gs], start=False, stop=True)
            TT(dT, psb(b), cnt, ALU.mult)
            b2 = 6
            mod0_rest(j)
            for g in range(4):
                gs = slice(g * 128, (g + 1) * 128)
                MM(ps[:, b2, gs], pw[:, g, :], dT[:, gs])
            CP(mixT[:, 8:12, j * 128:(j + 1) * 128], psb(b2).rearrange("p (g t) -> p g t", g=4), eng="act")
        DBG("poolT", mixT[:, 8:12, 0:128], [128, 4, 128])

        for j_ in range(NT, 12):
            mod0_rest(j_)
        mod_fin(0, 1)
        kvbufs = [A(32768, [128, NT, 384], BF16), A(186368, [128, NT, 384], BF16)]

        def wh_load(h):
            Wh = A((h % 2) * 12288, [128, 8, 768], BF16)
            for (c0, w_, o_) in ((h * 128, 128, 0), (512 + h * 128, 128, 128), (1024 + h * 256, 256, 256), (2048 + h * 256, 256, 512)):
                DMA(Wh[:, :, o_:o_ + w_], win_d[:, c0:c0 + w_].rearrange("(c p) f -> p c f", p=128), "wh%d" % (h % 2), eng="pool")

        def p1_items(h):
            Wh = A((h % 2) * 12288, [128, 8, 768], BF16)
            kv = kvbufs[h % 2]
            items = []
            for j in range(NT):
                def item(j=j):
                    b = 6 + (j % 2)
                    jsl = slice(j * 128, (j + 1) * 128)
                    for c in range(8):
                        MM(ps[:, b, 0:384], hT[:, c, jsl], Wh[:, c, 128:512], start=(c == 0), stop=(c == 7))
                    CP(kv[:, j, :], ps[:, b, 0:384], eng="act")
                items.append(item)
            return items

        kzb = A(147456, [128, T], BF16)
        wh_load(0)
        for it_ in p1_items(0):
            it_()
        TS(kzb.rearrange("p (c n) -> p c n", n=128), kvbufs[0][:, :, 0:128], zb[:, 0:1], None, ALU.mult)
        for h in range(4):
            Wh = A((h % 2) * 12288, [128, 8, 768], BF16)
            qT = A(24576, [128, T], BF16)
            kT = A(24576 + 2 * T, [128, T], BF16)
            kv = kvbufs[h % 2]
            sg = A(45056, [128, NT, 256], BF16)
            qxf = A(53248, [128, T], BF16)
            qxb = A(57344, [128, T], BF16)
            kzf = A(61440, [128, T], BF16)
            Sbs = A(151552, [128, NT, 256], BF16)
            SB0 = 159744
            so4 = [A(SB0 + k * 1024, [128, 256], F32) for k in range(4)]
            Sfbf = [A(SB0 + 4096 + k * 512, [128, 256], BF16) for k in range(2)]
            PTs = [A(SB0 + 5120 + k * 256, [128, 128], BF16) for k in range(3)]
            yns = [A(SB0 + 6144 + k * 1024, [128, 256], F32) for k in range(3)]
            rts = [A(SB0 + 9216 + k * 512, [128, 256], BF16) for k in range(2)]
            if h + 1 < 4:
                wh_load(h + 1)
            m1slots = [A(169984 + k * 4096, [128, 8, 256], BF16) for k in range(2)]
            if h > 0:
                for k in range(2):
                    mod_mm(1, ((h - 1) * 2 + k) * 256, 256, m1slots[k], 1, 128 + k * 2)
            for k in range(2):
                mod_dma(1, (h * 2 + k) * 256, 256, m1slots[k], "wa1_%d" % k)
            TS(kzf.rearrange("p (c n) -> p c n", n=128), kv[:, :, 0:128], zf[:, h:h + 1], None, ALU.mult)
            p2 = []
            for tb in range(T // 512):
                for (dst, o_) in ((qT, 0), (kT, 128)):
                    def item(tb=tb, dst=dst, o_=o_):
                        tsl = slice(tb * 512, (tb + 1) * 512)
                        b = (tb * 2 + (o_ // 128)) % 2
                        for c in range(8):
                            MM(psb(b), Wh[:, c, o_:o_ + 128], hT[:, c, tsl], start=(c == 0), stop=(c == 7))
                        CP(dst[:, tsl], psb(b), eng="act")
                    p2.append(item)
            for j in range(NT):
                def item(j=j):
                    b = 6 + (j % 2)
                    jsl = slice(j * 128, (j + 1) * 128)
                    for c in range(8):
                        MM(ps[:, b, 0:256], hT[:, c, jsl], Wh[:, c, 512:768], start=(c == 0), stop=(c == 7))
                    ACT(sg[:, j, :], ps[:, b, 0:256], AF.Silu)
                p2.append(item)
            DMA(so4[NT % 4], s0_d[1, h], "s0")
            npi = 0
            for i_ in range(NT):
                c = NT - 1 - i_
                csl = slice(c * 128, (c + 1) * 128)
                b = 2 + c % 2
                MM(ps[:, b, 0:256], kzb[:, csl], kv[:, c, 128:384])
                ACT(Sbs[:, c, :], so4[(c + 1) % 4], AF.Identity, scale=flags[:, 3, c:c + 1])
                STT(so4[c % 4], so4[(c + 1) % 4], kd[:, 1, c, h:h + 1], ps[:, b, 0:256], ALU.mult, ALU.add)
                if c % 2 == 0:
                    DMA(st_d[c // 2, 1, h], so4[c % 4], "so%d" % (c % 4))
                tgt = ((i_ + 1) * len(p2)) // NT
                while npi < tgt:
                    p2[npi]()
                    npi += 1
            TT(qxf.rearrange("p (c n) -> p c n", n=128), qT.rearrange("p (c n) -> p c n", n=128),
               xif[:, h:h + 1, :].to_broadcast([128, NT, 128]), ALU.mult)
            TT(qxb.rearrange("p (c n) -> p c n", n=128), qT.rearrange("p (c n) -> p c n", n=128),
               xib[:, h:h + 1, :].to_broadcast([128, NT, 128]), ALU.mult, eng="pool")
            DMA(so4[3], s0_d[0, h], "s0")
            ACT(Sfbf[0], so4[3], AF.Identity, scale=flags[:, 2, 0:1])

            def F0(c):
                csl = slice(c * 128, (c + 1) * 128)
                k2 = c % 2
                MM(ps[:, 2 + k2, 0:128], kT[:, csl], qT[:, csl])
                MM(ps[:, 2 + k2, 128:384], kzf[:, csl], kv[:, c, 128:384])
                TT(PTs[c % 3], ps[:, 2 + k2, 0:128], Mall[:, h, :], ALU.mult)

            def F1(c):
                csl = slice(c * 128, (c + 1) * 128)
                k2 = c % 2
                MM(ps[:, 4 + k2, 0:256], PTs[c % 3], kv[:, c, 128:384], start=True, stop=False)
                MM(ps[:, 4 + k2, 0:256], qxb[:, csl], Sbs[:, c, :], start=False, stop=False)
                MM(ps[:, 4 + k2, 0:256], qxf[:, csl], Sfbf[k2], start=False, stop=True)
                STT(so4[c % 4], so4[(c - 1) % 4], kd[:, 0, c, h:h + 1], ps[:, 2 + k2, 128:384], ALU.mult, ALU.add)
                if c % 2 == 1:
                    DMA(st_d[c // 2, 0, h], so4[c % 4], "so%d" % (c % 4))
                if c < NT - 1:
                    ACT(Sfbf[(c + 1) % 2], so4[c % 4], AF.Identity, scale=flags[:, 2, c + 1:c + 2])

            def F2(c):
                k2 = c % 2
                yn = yns[c % 3]
                sc0 = 16 + k2 * 8
                CP(yn, ps[:, 4 + k2, 0:256], eng="act")
                P.op("dve", lambda e, yn=yn, sc0=sc0: e.bn_stats(st8[:, sc0:sc0 + 6], yn), reads=[yn], writes=[st8[:, sc0:sc0 + 6]])
                P.op("dve", lambda e, sc0=sc0: e.bn_aggr(st8[:, sc0 + 6:sc0 + 8], st8[:, sc0:sc0 + 6]),
                     reads=[st8[:, sc0:sc0 + 6]], writes=[st8[:, sc0 + 6:sc0 + 8]])

            def F2b(c):
                k2 = c % 2
                yn = yns[c % 3]
                sc0 = 16 + k2 * 8
                ACT(st8[:, sc0 + 7:sc0 + 8], st8[:, sc0 + 7:sc0 + 8], AF.Sqrt, bias=EPS)
                RECIP(st8[:, sc0 + 7:sc0 + 8], st8[:, sc0 + 7:sc0 + 8])
                TS(yn, yn, st8[:, sc0 + 6:sc0 + 7], st8[:, sc0 + 7:sc0 + 8], ALU.subtract, ALU.mult)
                TT(rts[k2], yn, sg[:, c, :], ALU.mult, eng="pool")

            def F3(c):
                csl = slice(c * 128, (c + 1) * 128)
                k2 = c % 2
                pt = psbf(k2)
                for k in range(2):
                    TR(pt[:, k * 128:(k + 1) * 128], rts[k2][:, k * 128:(k + 1) * 128])
                CP(mixT[:, 2 * h:2 * h + 2, csl], pt[:, 0:256].rearrange("p (k t) -> p k t", k=2), eng="act")

            nxt = p1_items(h + 1) if h + 1 < 4 else []
            stages = [F0, F1, F2, F2b, F3]
            for step in range(NT + 4):
                for s_i, fn in enumerate(stages):
                    c_ = step - s_i
                    if 0 <= c_ < NT:
                        fn(c_)
                if step < len(nxt):
                    nxt[step]()
            for it_ in nxt[NT + 4:]:
                it_()
            if h + 1 < 4:
                TS(kzb.rearrange("p (c n) -> p c n", n=128), kvbufs[(h + 1) % 2][:, :, 0:128], zb[:, h + 1:h + 2], None, ALU.mult)
        for k in range(2):
            mod_mm(1, (6 + k) * 256, 256, A(169984 + k * 4096, [128, 8, 256], BF16), 1, 128 + k * 2)
        mod_fin(1, 0)
        DBG("mixT", mixT[:, :, 0:128], [128, 12, 128])

        build_ggb(0, 192512)
        woe = A(147456, [128, 12, 1024], BF16)
        wst = [A(81920 + k * 4096, [128, 1024], F32) for k in range(2)]
        ffn_prefetch(0)
        for kc in range(12):
            DMA(wst[kc % 2], woe_d[kc * 128:(kc + 1) * 128, :], "woe%d" % (kc % 2))
            TS(woe[:, kc, :], wst[kc % 2], gnpp[:, kc:kc + 1], None, ALU.mult)
        for j in range(NT):
            DMA(X[:, j, :], x_d[j * 128:(j + 1) * 128, :], "xre")
        nst = ffn_norm_stages(0, 0)
        nt_h = min(T, 1024) // 128
        for j in range(NT):
            banks = (4 + (j % 2) * 2, 5 + (j % 2) * 2)
            for hf in range(2):
                for kc in range(12):
                    MM(psb(banks[hf]), mixT[:, kc, j * 128:(j + 1) * 128], woe[:, kc, hf * 512:(hf + 1) * 512],
                       start=(kc == 0), stop=(kc == 11))
            post_norm(j, banks, 0, 90112)
            skewed(nst, j - 1, nt_h)
        for j in range(NT, nt_h + 3):
            skewed(nst, j - 1, nt_h)
        DBG("x1", X[:, 0, :], [128, 1024])
        ffn(0)
        DBG("x2", X[:, 0, :], [128, 1024])

        cur["pp"] = pps[1]
        hT = A(65536, [128, 8, T], BF16)
        wqkv = A(98304, [128, 8, 1536], BF16)
        qTa = A(122880, [128, 8, T], BF16)
        kTa = A(155648, [128, 2, NKC * 128], BF16)
        Vt = A(165888, [128, NKC, 256], BF16)
        qn = A(176128, [128, 10, 128], F32)
        T2 = A(181248, [128, 10, 128], F32)
        qr = A(186368, [128, 10, 128], BF16)
        ropes = [A(188928 + k * 1024, [128, 256], F32) for k in range(2)]
        kvst = [A(190976 + k * 2048, [128, 512], F32) for k in range(2)]
        gains = A(195072, [128, 2, 128], F32)
        ckbf = A(196096, [128, 4, 256], BF16)
        for k3 in range(3):
            DMA(wqkv[:, :, k3 * 512:(k3 + 1) * 512], wqkv_d[:, k3 * 512:(k3 + 1) * 512].rearrange("(c p) f -> p c f", p=128), "wqkv", eng="pool")
        pipeline(NT, [lambda j: norm_A1(X[:, j, :], j, 176128), lambda j: norm_A2(X[:, j, :], j, 176128), lambda j: norm_B(hT, j, 0, 176128)])
        grow = A(181248, [1, 256], F32, 0, 1)
        DMA(grow, qkg_d, "c3")
        MM(ps[:, 7, 0:256], one1[0:1, 0:128], grow[0:1, :])
        CP(gains.rearrange("p a b -> p (a b)"), ps[:, 7, 0:256])
        DMA(ckbf, ck_d.rearrange("(c p) f -> p c f", p=128), "ckv", eng="pool")
        DMA(Vt[:, 0:NPAST, :], cv_d.rearrange("(c p) f -> p c f", p=128), "ckv", eng="pool")
        for c in range(NPAST):
            pt = psbf(2 + c % 2)
            for kh in range(2):
                TR(pt[:, kh * 128:(kh + 1) * 128], ckbf[:, c, kh * 128:(kh + 1) * 128])
            CP(kTa[:, :, c * 128:(c + 1) * 128], pt[:, 0:256].rearrange("p (k t) -> p k t", k=2), eng="act")
        sqj = A(196096, [128, 128], BF16)

        def Q0(j):
            jsl = slice(j * 128, (j + 1) * 128)
            b0 = 2 + (j % 2) * 3
            for k3 in range(3):
                for c in range(8):
                    MM(psb(b0 + k3), hT[:, c, jsl], wqkv[:, c, k3 * 512:(k3 + 1) * 512], start=(c == 0), stop=(c == 7))

        qns = [qn, A(198656, [128, 10, 128], F32)]

        def Qa(j):
            b0 = 2 + (j % 2) * 3
            so_ = 32 + (j % 2) * 16
            for hh in range(10):
                src = ps[:, b0 + hh // 4, (hh % 4) * 128:(hh % 4 + 1) * 128]
                ACT(sqj, src, AF.Square, scale=128.0 ** -0.5, accum=st8[:, so_ + hh:so_ + hh + 1])
            ACT(st8[:, so_:so_ + 10], st8[:, so_:so_ + 10], AF.Sqrt, bias=EPS)
            RECIP(st8[:, so_:so_ + 10], st8[:, so_:so_ + 10])

        def Qb(j):
            qn_ = qns[j % 2]
            jsl = slice(j * 128, (j + 1) * 128)
            b0 = 2 + (j % 2) * 3
            so_ = 32 + (j % 2) * 16
            DMA(ropes[j % 2], rope_d[jsl, :], "rope%d" % (j % 2))
            for hh in range(10):
                src = ps[:, b0 + hh // 4, (hh % 4) * 128:(hh % 4 + 1) * 128]
                STT(qn_[:, hh, :], src, st8[:, so_ + hh:so_ + hh + 1], gains[:, (0 if hh < 8 else 1), :], ALU.mult, ALU.mult)
            ks = kvst[j % 2]
            CP(Vt[:, NPAST + j, :], ps[:, b0 + 2, 256:512])
            CP(ks[:, 256:512], ps[:, b0 + 2, 256:512], eng="act")
            DMA(nk_d[jsl, :].rearrange("p (a b) -> p a b", a=2), qn_[:, 8:10, :], "okv%d" % (j % 2))
            DMA(nv_d[jsl, :], ks[:, 256:512], "okv%d" % (j % 2))

        def Qc(j):
            qn_ = qns[j % 2]
            rp = ropes[j % 2]
            qn5 = qn_.rearrange("p h (a s d) -> p h a s d", a=2, s=2)
            t25 = T2.rearrange("p h (a s d) -> p h a s d", a=2, s=2)
            sn5 = rp[:, 128:256].rearrange("p (a s d) -> p a s d", a=2, s=2)
            for s_ in range(2):
                TT(t25[:, :, :, s_, :], qn5[:, :, :, 1 - s_, :],
                   sn5[:, :, s_, :].unsqueeze(1).to_broadcast([128, 10, 2, 32]), ALU.mult, eng="pool")
            TT(qn_, qn_, rp[:, 0:128].unsqueeze(1).to_broadcast([128, 10, 128]), ALU.mult)
            TT(qr, qn_, T2, ALU.add)

        def Qd(j):
            jsl = slice(j * 128, (j + 1) * 128)
            pt = psbf(0)
            for hh in range(8):
                TR(pt[:, hh * 128:(hh + 1) * 128], qr[:, hh, :])
            pt2 = psbf(1)
            for kh in range(2):
                TR(pt2[:, kh * 128:(kh + 1) * 128], qr[:, 8 + kh, :])
            CP(qTa[:, :, jsl], pt.rearrange("p (h t) -> p h t", h=8), eng="act")
            CP(kTa[:, :, (NPAST + j) * 128:(NPAST + j + 1) * 128], pt2[:, 0:256].rearrange("p (k t) -> p k t", k=2), eng="act")

        for step in range(NT + 4):
            for fn, lag in ((Qd, 4), (Qb, 2), (Qc, 3), (Qa, 1), (Q0, 0)):
                if 0 <= step - lag < NT:
                    fn(step - lag)
        DBG("qT", qTa[:, :, 0:128], [128, 8, 128])
        DBG("kT", kTa[:, :, 0:640], [128, 2, 640])

        oT = A(65536, [128, 8, T], BF16)
        wo = A(98304, [128, 8, 1024], BF16)
        DMA(wo, wo_d.rearrange("(c p) f -> p c f", p=128), "wo", eng="pool")
        NPT = 8
        PTr = [A(176128 + k * 1024, [128, 512], BF16) for k in range(NPT)]
        rinv = [A(198656 + k * 2048, [128, 512], F32) for k in range(2)]
        SC = 128.0 ** -0.5
        steps = [(kvh, pr, sq, kc) for kvh in range(2) for pr in range(2) for sq in range(NSEQ) for kc in range(NKC)]
        NS = len(steps)
        accs = [A(114688 + k * 2048, [128, 512], F32) for k in range(2)]

        def att_S(i):
            kvh, pr, sq, kc = steps[i]
            n0 = kvh * 4 + pr * 2
            MM(psb(i % 3).rearrange("p (h t) -> p h t", h=2), kTa[:, kvh, kc * 128:(kc + 1) * 128],
               qTa[:, n0:n0 + 2, sq * 256:(sq + 1) * 256])

        def att_E(i):
            kvh, pr, sq, kc = steps[i]
            col = kc * NSEQ + sq
            ACT(PTr[i % NPT], psb(i % 3), AF.Exp, scale=SC, bias=maskb[:, col:col + 1])

        def att_V(i):
            kvh, pr, sq, kc = steps[i]
            n0 = kvh * 4 + pr * 2
            it = i // NKC
            bo = 4 + (it % 2) * 2
            bsum = bo + 1
            pt_ = PTr[i % NPT]
            acc = accs[it % 2]
            MM(psb(bo), Vt[:, kc, kvh * 128:(kvh + 1) * 128], pt_, start=(kc == 0), stop=(kc == NKC - 1))
            if kc % 4 != 1:
                MM(psb(bsum), ones_bf[:], pt_, start=(kc == 0), stop=False)
            elif kc == 1:
                CP(acc, pt_)
            else:
                TT(acc, acc, pt_, ALU.add)
            if kc == NKC - 1:
                MM(psb(bsum), onesf[:], acc, start=False, stop=True)
                ri = rinv[it % 2]
                RECIP(ri, psb(bsum))
                TT(oT[:, n0:n0 + 2, sq * 256:(sq + 1) * 256], psb(bo).rearrange("p (h t) -> p h t", h=2),
                   ri.rearrange("p (h t) -> p h t", h=2), ALU.mult)

        LOOK = 2
        for i in range(min(LOOK, NS)):
            att_S(i)
        m2slots = [A(186368 + k * 4096, [128, 8, 256], BF16) for k in range(3)]
        NB2 = 16
        dma_at = {(n_ * NS) // (NB2 + 1): n_ for n_ in range(NB2)}
        assert len(dma_at) == NB2
        for i in range(NS):
            if i + LOOK < NS:
                att_S(i + LOOK)
            att_E(i)
            att_V(i)
            if i in dma_at:
                n_ = dma_at[i]
                if n_ >= 1:
                    mod_mm(1, 2048 + (n_ - 1) * 256, 256, m2slots[(n_ - 1) % 3], 3, 128 + ((n_ - 1) % 3) * 2)
                mod_dma(1, 2048 + n_ * 256, 256, m2slots[n_ % 3], "wa2_%d" % (n_ % 3))
        mod_mm(1, 2048 + (NB2 - 1) * 256, 256, m2slots[(NB2 - 1) % 3], 3, 128 + ((NB2 - 1) % 3) * 2)
        mod_fin(1, 1)
        build_ggb(1, 176128)
        DBG("oT", oT[:, :, 0:128], [128, 8, 128])

        nst = ffn_norm_stages(1, 0)
        ffn_prefetch(1)
        for j in range(NT):
            banks = (4 + (j % 2) * 2, 5 + (j % 2) * 2)
            for hf in range(2):
                for kc in range(8):
                    MM(psb(banks[hf]), oT[:, kc, j * 128:(j + 1) * 128], wo[:, kc, hf * 512:(hf + 1) * 512],
                       start=(kc == 0), stop=(kc == 7))
            post_norm(j, banks, 0, 114688)
            skewed(nst, j - 1, nt_h)
        for j in range(NT, nt_h + 3):
            skewed(nst, j - 1, nt_h)
        DBG("x3", X[:, 0, :], [128, 1024])
        ffn(1)
        for j in range(NT):
            DMA(y_d[j * 128:(j + 1) * 128, :], X[:, j, :], "yout")

        finals = ["yout", "okv0", "okv1", "so0", "so1", "so2", "so3"] + (["dbg"] if dbg and "dbg" in P.dcount else [])
        emit(P, nc, es, final_dma_keys=finals)
    return nc, dbg_outs


def make_consts(NT, sample):
    T = NT * 128
    NKC = NPAST + NT
    NSEQ = NT // 2
    f32 = np.float32
    c = {}
    c["ident"] = np.eye(128, dtype=f32)
    m = np.arange(128)[:, None].astype(f32)
    n = np.arange(128)[None, :].astype(f32)
    BIG = 1.0e6
    EF = np.where(n >= m, n - m, BIG)
    EB = np.where(m >= n, m - n, BIG)
    N1 = np.broadcast_to(n + 1.0, (128, 128))
    N2 = np.broadcast_to(128.0 - n, (128, 128))
    c["retE"] = np.ascontiguousarray(np.stack([EF, EB, N1, N2], axis=1)).astype(f32)
    c["retZ"] = np.stack([127.0 - np.arange(128), np.arange(128)], axis=1).astype(f32)
    wins = (2, 4, 8, 16)
    Base = np.zeros((128, 4, 128), f32)
    Dp = np.zeros((128, 4, 128), f32)
    Dn = np.zeros((128, 4, 128), f32)
    Bp = np.zeros((128, 4, 128), f32)
    Bn = np.zeros((128, 4, 128), f32)
    ccur = np.zeros((4, 128), f32)
    cprev = np.zeros((4, 128), f32)
    cnext = np.zeros((4, 128), f32)
    for g, w in enumerate(wins):
        hlf = w // 2
        for t in range(128):
            for s in range(t - hlf, t + hlf):
                if s < 0:
                    Bp[s + 128, g, t] += 1
                    cprev[g, t] += 1
                elif s >= 128:
                    Bn[s - 128, g, t] += 1
                    cnext[g, t] += 1
                else:
                    Base[s, g, t] += 1
                    ccur[g, t] += 1
            Base[t, g, t] -= ccur[g, t]
            Dp[t, g, t] = -cprev[g, t]
            Dn[t, g, t] = -cnext[g, t]
    c["poolB"] = np.ascontiguousarray(np.stack([Base, Dp, Dn, Bp, Bn], axis=1).reshape(128, 5, 512))
    pc = np.stack([ccur.reshape(512), cprev.reshape(512), cnext.reshape(512)], axis=0)
    c["poolC"] = np.ascontiguousarray(np.broadcast_to(pc[None], (128, 3, 512))).astype(f32)
    j = np.arange(NT)
    if sample:
        hp = (j > 0)
        hn = (j < NT - 1)
        keepf = np.ones(NT)
        keepb = np.ones(NT)
    else:
        hp = (j % 2 == 1)
        hn = (j % 2 == 0)
        keepf = (j % 2 == 1)
        keepb = (j % 2 == 0)
    fl = np.stack([hp, hn, keepf, keepb], axis=0).astype(f32)
    c["flags"] = np.ascontiguousarray(np.broadcast_to(fl[None], (128, 4, NT))).astype(f32)
    mb = np.zeros((NKC, NSEQ), f32)
    if not sample:
        mb[:] = -30000.0
        for kc in range(NPAST, NKC):
            mb[kc, (kc - NPAST) // 2] = 0.0
    c["maskb"] = np.ascontiguousarray(np.broadcast_to(mb.reshape(1, -1), (128, NKC * NSEQ))).astype(f32)
    rope = np.zeros((T, 256), f32)
    if sample:
        t = np.arange(T)
        r = (t // 64).astype(np.float64)
        cl = (t % 64).astype(np.float64)
        freqs = 10000.0 ** (-np.arange(32, dtype=np.float64) / 32.0)
        for a, pos in enumerate((r, cl)):
            ang = pos[:, None] * freqs[None, :]
            cs = np.cos(ang).astype(f32)
            sn = np.sin(ang).astype(f32)
            rope[:, a * 64:a * 64 + 32] = cs
            rope[:, a * 64 + 32:a * 64 + 64] = cs
            rope[:, 128 + a * 64:128 + a * 64 + 32] = -sn
            rope[:, 128 + a * 64 + 32:128 + a * 64 + 64] = sn
    else:
        rope[:, 0:128] = 1.0
    c["rope"] = rope
    return c


_NC_CACHE = {}


def run_cores(NT, core_inputs, dbg=False, runner=None):
    key = (NT, dbg)
    if key not in _NC_CACHE:
        _NC_CACHE[key] = build(NT, dbg)
    nc, dbg_outs = _NC_CACHE[key]
    if runner is not None:
        return runner(nc, core_inputs)
    res = run_bass_kernel_spmd(nc, core_inputs, core_ids=list(range(len(core_inputs))))
    return res.results


def prep_core(NT, sample, xs, ccv, s0, ck, cv, W):
    f32 = np.float32
    m = dict(W)
    m.update(make_consts(NT, sample))
    m["x"] = np.ascontiguousarray(xs, dtype=f32)
    m["cc"] = np.ascontiguousarray(ccv.reshape(128, 8), dtype=f32)
    m["s0"] = np.ascontiguousarray(s0, dtype=f32)
    m["ck"] = np.ascontiguousarray(ck.reshape(512, 256), dtype=f32)
    m["cv"] = np.ascontiguousarray(cv.reshape(512, 256), dtype=f32)
    return m


def prep_weights(w_ada, b_ada, norm_gain, w_ffn_in, w_ffn_out, w_in_even, ret_decay_logit, ret_gn_gain, pool_w,
                 pool_scale, w_out_even, w_qkv, q_norm_gain, k_norm_gain, w_o):
    f32 = np.float32
    ca = lambda a: np.ascontiguousarray(a, dtype=f32)
    return {
        "w_ada": ca(w_ada), "b_ada": ca(b_ada), "norm_gain": ca(norm_gain.reshape(2, 4 * D)),
        "w_ffn_in": ca(w_ffn_in), "w_ffn_out": ca(w_ffn_out), "w_in_even": ca(w_in_even[0]),
        "dlog": ca(ret_decay_logit.reshape(1, 8)),
        "gn": ca(ret_gn_gain[0].reshape(8, 128).T),
        "pool_w": ca(pool_w[0]), "pscale": ca(pool_scale[0].reshape(4, 128).T),
        "w_out_even": ca(w_out_even[0]), "w_qkv": ca(w_qkv[0]),
        "qkg": ca(np.concatenate([q_norm_gain[0], k_norm_gain[0]]).reshape(1, 256)),
        "w_o": ca(w_o[0]),
    }


def kernel(x_prompt, x_sample, state_ret, cache_k, cache_v, c, c_ctx, w_ada, b_ada, norm_gain,
           w_ffn_in, w_ffn_out, w_in_even, ret_decay_logit, ret_gn_gain, pool_w, pool_scale,
           w_out_even, w_qkv, q_norm_gain, k_norm_gain, w_o, _runner=None, _dbg=False):
    x_prompt = np.asarray(x_prompt)
    x_sample = np.asarray(x_sample)
    B, S, _ = x_prompt.shape
    DB, DS, _ = x_sample.shape
    NT = DS // 128
    npc = (B * S) // (NT * 128)
    spc = B // npc
    W = prep_weights(*[np.asarray(a) for a in (w_ada, b_ada, norm_gain, w_ffn_in, w_ffn_out, w_in_even, ret_decay_logit,
                                               ret_gn_gain, pool_w, pool_scale, w_out_even, w_qkv, q_norm_gain,
                                               k_norm_gain, w_o)])
    f32 = np.float32
    cores = []
    zs = np.zeros((2, 4, 128, 256), f32)
    zk = np.zeros((512, 256), f32)
    for i in range(npc):
        xs = x_prompt[i * spc:(i + 1) * spc].reshape(NT * 128, D)
        cores.append(prep_core(NT, False, xs, np.asarray(c_ctx), zs, zk, zk, W))
    state_ret = np.asarray(state_ret)
    cache_k = np.asarray(cache_k)
    cache_v = np.asarray(cache_v)
    for b in range(DB):
        cores.append(prep_core(NT, True, x_sample[b], np.asarray(c)[b], state_ret[b, 0], cache_k[b, 0], cache_v[b, 0], W))
    res = run_cores(NT, cores, dbg=_dbg, runner=_runner)
    y_p = np.concatenate([res[i]["y"].reshape(spc, S, D) for i in range(npc)], axis=0)
    y_s = np.stack([res[npc + b]["y"] for b in range(DB)], axis=0)
    st = np.concatenate([res[i]["st"] for i in range(npc)], axis=0)[:, None]
    nk = np.concatenate([res[i]["nk"].reshape(spc, S, 2, 128) for i in range(npc)], axis=0)[:, None]
    nv = np.concatenate([res[i]["nv"].reshape(spc, S, 2, 128) for i in range(npc)], axis=0)[:, None]
    out = (y_p.astype(f32), y_s.astype(f32), st.astype(f32), nk.astype(f32), nv.astype(f32))
    if _dbg:
        return out, res
    return out
```
